# Optimizing a Trainium2 kernel written in Bass

```python
import math
import jax
import jax.numpy as jnp
from jax import lax
import numpy as np


D_MODEL = 2048
BATCH = 2
SEQ = 4096
DEPTH = 4
DEC_BATCH = 8
DEC_SEQ = 2048
PAST_LEN = 128

GRID_W = 64
Q_BLOCK = 128
NORM_EPS = 1e-6

SSD_HEADS = 16
SSD_HEAD_DIM = 64
SSD_WIDTH = SSD_HEADS * SSD_HEAD_DIM
SSD_GROUPS = 2
SSD_STATE = 128
SSD_CONV_W = 5
SSD_CHUNK = 128
SSD_CONV_DIM = SSD_WIDTH + 2 * SSD_GROUPS * SSD_STATE

GQA_HEADS = 4
GQA_KV_HEADS = 2
GQA_HEAD_DIM = 128
GQA_WIDTH = GQA_HEADS * GQA_HEAD_DIM
ROPE_THETA = 10000.0

DIFF_HEADS = 4
DIFF_QK_DIM = 64
DIFF_V_DIM = 128
DIFF_WIDTH = DIFF_HEADS * DIFF_V_DIM

MIX_WIDTH = SSD_WIDTH + GQA_WIDTH + DIFF_WIDTH
D_FF = -(-8 * D_MODEL // (3 * 256)) * 256

IN_SIZES = (
    SSD_WIDTH,
    SSD_CONV_DIM,
    2 * SSD_HEADS,
    GQA_WIDTH,
    GQA_KV_HEADS * GQA_HEAD_DIM,
    GQA_KV_HEADS * GQA_HEAD_DIM,
    DIFF_HEADS * 2 * DIFF_QK_DIM,
    DIFF_HEADS * 2 * DIFF_QK_DIM,
    DIFF_WIDTH,
)
IN_COLS = sum(IN_SIZES)

kernel_name = 'hybrid_bidir_ssd_gqa_diff_encoder'

F32 = jnp.float32


def rmsnorm(x, g):
    xf = x.astype(F32)
    y = xf * lax.rsqrt(jnp.mean(xf * xf, axis=-1, keepdims=True) + NORM_EPS)
    return (y * g.astype(F32)).astype(x.dtype)


def ada_modulation(c, w, b):
    m = jnp.einsum('bd,de->be', jax.nn.silu(c), w) + b
    return [t[:, None, :] for t in jnp.split(m, 6, axis=-1)]


def centred_depthwise_conv(x, w, b):
    k, ch = w.shape
    pad = k // 2
    y = lax.conv_general_dilated(
        x, w[:, None, :], window_strides=(1,), padding=[(pad, pad)],
        dimension_numbers=('NWC', 'WIO', 'NWC'), feature_group_count=ch)
    return y + b


def segsum_exp(a):
    cs = jnp.cumsum(a, axis=-1)
    diff = cs[..., :, None] - cs[..., None, :]
    n = a.shape[-1]
    mask = jnp.tril(jnp.ones((n, n), dtype=bool))
    return jnp.exp(jnp.where(mask, diff, -jnp.inf))


def ssd_scan(x, dt, a, bm, cm):
    b, s, h, p = x.shape
    g, n = bm.shape[-2:]
    r = h // g
    nc, l = s // SSD_CHUNK, SSD_CHUNK
    xc = (x * dt[..., None]).reshape(b, nc, l, g, r, p)
    ac = jnp.moveaxis((dt * a).reshape(b, nc, l, g, r), 2, -1)
    bc = bm.reshape(b, nc, l, g, n)
    cc = cm.reshape(b, nc, l, g, n)
    a_cs = jnp.cumsum(ac, axis=-1)
    decay = segsum_exp(ac)
    scores = jnp.einsum('bclgn,bcsgn->bcgls', cc, bc)
    y_diag = jnp.einsum('bcgrls,bcsgrp->bclgrp', scores[:, :, :, None] * decay, xc)
    decay_to_end = jnp.moveaxis(jnp.exp(a_cs[..., -1:] - a_cs), -1, 2)
    chunk_states = jnp.einsum('bcsgn,bcsgrp->bcgrpn', bc, xc * decay_to_end[..., None])
    chunk_decay = jnp.exp(a_cs[..., -1])

    def step(carry, inp):
        st, dec = inp
        return carry * dec[..., None, None] + st, carry

    init = jnp.zeros((b, g, r, p, n), dtype=chunk_states.dtype)
    _, prev = lax.scan(step, init, (jnp.moveaxis(chunk_states, 1, 0),
                                    jnp.moveaxis(chunk_decay, 1, 0)))
    prev = jnp.moveaxis(prev, 0, 1)
    decay_in = jnp.moveaxis(jnp.exp(a_cs), -1, 2)
    y_off = jnp.einsum('bclgn,bcgrpn->bclgrp', cc, prev) * decay_in[..., None]
    return (y_diag + y_off).reshape(b, s, h, p)


def ssd_mixer(z, xbc, dt_raw, conv_w, conv_b, dt_bias, a_log, d_skip, norm_g):
    b, s, _ = xbc.shape
    xbc = jax.nn.silu(centred_depthwise_conv(xbc, conv_w, conv_b))
    xs, bm, cm = jnp.split(xbc, [SSD_WIDTH, SSD_WIDTH + SSD_GROUPS * SSD_STATE], axis=-1)
    xs = xs.reshape(b, s, SSD_HEADS, SSD_HEAD_DIM).astype(F32)
    bm = bm.reshape(b, s, SSD_GROUPS, SSD_STATE).astype(F32)
    cm = cm.reshape(b, s, SSD_GROUPS, SSD_STATE).astype(F32)
    dt = jax.nn.softplus(dt_raw.astype(F32).reshape(b, s, 2, SSD_HEADS)
                         + dt_bias.astype(F32))
    a = -jnp.exp(a_log.astype(F32))
    y_fwd = ssd_scan(xs, dt[:, :, 0], a[0], bm, cm)
    flip = lambda t: jnp.flip(t, axis=1)
    y_bwd = flip(ssd_scan(flip(xs), flip(dt[:, :, 1]), a[1], flip(bm), flip(cm)))
    y = y_fwd + y_bwd + xs * d_skip.astype(F32)[:, None]
    y = y.reshape(b, s, SSD_WIDTH) * jax.nn.silu(z.astype(F32))
    yg = y.reshape(b, s, SSD_GROUPS, SSD_WIDTH // SSD_GROUPS)
    yg = yg * lax.rsqrt(jnp.mean(yg * yg, axis=-1, keepdims=True) + NORM_EPS)
    return (yg.reshape(b, s, SSD_WIDTH) * norm_g.astype(F32)).astype(z.dtype)


def axial_rope_angles(seq):
    rows = seq // GRID_W
    row_idx, col_idx = jnp.meshgrid(jnp.arange(rows), jnp.arange(GRID_W), indexing='ij')
    row_idx = row_idx.reshape(-1).astype(F32)
    col_idx = col_idx.reshape(-1).astype(F32)
    axis_dim = GQA_HEAD_DIM // 2
    inv_freq = ROPE_THETA ** (-jnp.arange(0, axis_dim, 2, dtype=F32) / axis_dim)
    return row_idx[:, None] * inv_freq, col_idx[:, None] * inv_freq


def rope_axis(x, ang):
    cos = jnp.cos(ang)[:, None, :].astype(x.dtype)
    sin = jnp.sin(ang)[:, None, :].astype(x.dtype)
    x1, x2 = jnp.split(x, 2, axis=-1)
    return jnp.concatenate([x1 * cos - x2 * sin, x2 * cos + x1 * sin], axis=-1)


def axial_rope(x, ang_r, ang_c):
    xr, xc = jnp.split(x, 2, axis=-1)
    return jnp.concatenate([rope_axis(xr, ang_r), rope_axis(xc, ang_c)], axis=-1)


def gqa_attention(q, k, v):
    b, s, hq, d = q.shape
    r = hq // GQA_KV_HEADS
    nb = s // Q_BLOCK
    qb = jnp.moveaxis(q.reshape(b, nb, Q_BLOCK, GQA_KV_HEADS, r, d), 1, 0)
    scale = d ** -0.5

    def block(qblk):
        sc = jnp.einsum('blgrd,bsgd->bgrls', qblk, k, preferred_element_type=F32) * scale
        p = jax.nn.softmax(sc, axis=-1)
        return jnp.einsum('bgrls,bsgd->blgrd', p.astype(v.dtype), v)

    o = lax.map(block, qb)
    return jnp.moveaxis(o, 0, 1).reshape(b, s, hq * d)


def diff_attention(q, k, v, lam, slopes):
    b, s, h, _, dk = q.shape
    dv = v.shape[-1]
    nb = s // Q_BLOCK
    qb = jnp.moveaxis(q.reshape(b, nb, Q_BLOCK, h, 2, dk), 1, 0)
    kpos = jnp.arange(s, dtype=F32)
    qpos = kpos.reshape(nb, Q_BLOCK)
    scale = dk ** -0.5

    def block(args):
        qblk, qp = args
        sc = jnp.einsum('blhmd,bshmd->bhmls', qblk, k, preferred_element_type=F32) * scale
        alibi = -slopes[:, None, None] * jnp.abs(qp[:, None] - kpos[None, :])
        p = jax.nn.softmax(sc + alibi[None, :, None], axis=-1)
        attn = p[:, :, 0] - lam * p[:, :, 1]
        return jnp.einsum('bhls,bshe->blhe', attn.astype(v.dtype), v)

    o = lax.map(block, (qb, qpos))
    return jnp.moveaxis(o, 0, 1).reshape(b, s, h, dv)


def encoder_layer(x, c, ang_r, ang_c, li, w_mod, b_mod, norm1_g, w_in, conv_w, conv_b,
                  dt_bias, a_log, d_skip, ssd_norm_g, q_norm_g, k_norm_g, diff_lambda,
                  diff_subln_g, w_out, norm2_g, w_gate, w_up, w_down):
    b, s, _ = x.shape
    sh1, sc1, g1, sh2, sc2, g2 = ada_modulation(c, w_mod, b_mod)

    h = rmsnorm(x, norm1_g) * (1.0 + sc1) + sh1
    proj = jnp.einsum('bsd,de->bse', h, w_in)
    offsets = np.cumsum(IN_SIZES)[:-1].tolist()
    z, xbc, dt_raw, gq, gk, gv, dq, dk, dv = jnp.split(proj, offsets, axis=-1)

    y_ssd = ssd_mixer(z, xbc, dt_raw, conv_w, conv_b, dt_bias, a_log, d_skip, ssd_norm_g)

    gq = axial_rope(rmsnorm(gq.reshape(b, s, GQA_HEADS, GQA_HEAD_DIM), q_norm_g), ang_r, ang_c)
    gk = axial_rope(rmsnorm(gk.reshape(b, s, GQA_KV_HEADS, GQA_HEAD_DIM), k_norm_g), ang_r, ang_c)
    gv = gv.reshape(b, s, GQA_KV_HEADS, GQA_HEAD_DIM)
    y_gqa = gqa_attention(gq, gk, gv)

    lam_init = 0.8 - 0.6 * math.exp(-0.3 * li)
    lp = diff_lambda.astype(F32)
    lam = jnp.exp(jnp.sum(lp[0] * lp[1])) - jnp.exp(jnp.sum(lp[2] * lp[3])) + lam_init
    slopes = 2.0 ** (-8.0 * jnp.arange(1, DIFF_HEADS + 1, dtype=F32) / DIFF_HEADS)
    o = diff_attention(dq.reshape(b, s, DIFF_HEADS, 2, DIFF_QK_DIM),
                       dk.reshape(b, s, DIFF_HEADS, 2, DIFF_QK_DIM),
                       dv.reshape(b, s, DIFF_HEADS, DIFF_V_DIM), lam, slopes)
    y_diff = (rmsnorm(o, diff_subln_g) * (1.0 - lam_init)).reshape(b, s, DIFF_WIDTH)

    mix = jnp.concatenate([y_ssd, y_gqa, y_diff], axis=-1)
    x = x + g1 * jnp.einsum('bse,ed->bsd', mix, w_out)

    h = rmsnorm(x, norm2_g) * (1.0 + sc2) + sh2
    f = jax.nn.silu(jnp.einsum('bsd,df->bsf', h, w_gate)) * jnp.einsum('bsd,df->bsf', h, w_up)
    return x + g2 * jnp.einsum('bsf,fd->bsd', f, w_down)


def encoder(x, c, w_mod, b_mod, norm1_g, w_in, conv_w, conv_b, dt_bias, a_log, d_skip,
            ssd_norm_g, q_norm_g, k_norm_g, diff_lambda, diff_subln_g, w_out, norm2_g,
            w_gate, w_up, w_down, final_g):
    ang_r, ang_c = axial_rope_angles(x.shape[1])
    for li in range(DEPTH):
        x = encoder_layer(x, c, ang_r, ang_c, li, w_mod[li], b_mod[li], norm1_g[li], w_in[li],
                          conv_w[li], conv_b[li], dt_bias[li], a_log[li], d_skip[li],
                          ssd_norm_g[li], q_norm_g[li], k_norm_g[li], diff_lambda[li],
                          diff_subln_g[li], w_out[li], norm2_g[li], w_gate[li], w_up[li],
                          w_down[li])
    return rmsnorm(x, final_g)


def setup_inputs(seed: int = 0) -> dict:
    key = jax.random.key(seed)
    ks = jax.random.split(key, 26)
    nrm = lambda k, shape, scale: jax.random.normal(k, shape, F32) * scale
    gain = lambda k, shape: 1.0 + nrm(k, shape, 0.02)
    dt0 = jnp.exp(jax.random.uniform(ks[9], (DEPTH, 2, SSD_HEADS), F32,
                                     math.log(1e-3), math.log(1e-1)))
    return {
        'x_prompt': nrm(ks[0], (BATCH, SEQ, D_MODEL), 1.0),
        'x_sample': nrm(ks[1], (DEC_BATCH, DEC_SEQ, D_MODEL), 1.0),
        'c_prompt': nrm(ks[2], (BATCH, D_MODEL), 1.0),
        'c_sample': nrm(ks[3], (DEC_BATCH, D_MODEL), 1.0),
        'w_mod': nrm(ks[4], (DEPTH, D_MODEL, 6 * D_MODEL), 0.5 * D_MODEL ** -0.5),
        'b_mod': nrm(ks[5], (DEPTH, 6 * D_MODEL), 0.01),
        'norm1_g': gain(ks[6], (DEPTH, D_MODEL)),
        'w_in': nrm(ks[7], (DEPTH, D_MODEL, IN_COLS), D_MODEL ** -0.5),
        'conv_w': nrm(ks[8], (DEPTH, SSD_CONV_W, SSD_CONV_DIM), SSD_CONV_W ** -0.5),
        'conv_b': nrm(ks[10], (DEPTH, SSD_CONV_DIM), 0.01),
        'dt_bias': dt0 + jnp.log(-jnp.expm1(-dt0)),
        'a_log': jnp.log(jax.random.uniform(ks[11], (DEPTH, 2, SSD_HEADS), F32, 1.0, 16.0)),
        'd_skip': gain(ks[12], (DEPTH, SSD_HEADS)),
        'ssd_norm_g': gain(ks[13], (DEPTH, SSD_WIDTH)),
        'q_norm_g': gain(ks[14], (DEPTH, GQA_HEAD_DIM)),
        'k_norm_g': gain(ks[15], (DEPTH, GQA_HEAD_DIM)),
        'diff_lambda': nrm(ks[16], (DEPTH, 4, DIFF_QK_DIM), 0.1),
        'diff_subln_g': gain(ks[17], (DEPTH, DIFF_V_DIM)),
        'w_out': nrm(ks[18], (DEPTH, MIX_WIDTH, D_MODEL), MIX_WIDTH ** -0.5),
        'norm2_g': gain(ks[19], (DEPTH, D_MODEL)),
        'w_gate': nrm(ks[20], (DEPTH, D_MODEL, D_FF), D_MODEL ** -0.5),
        'w_up': nrm(ks[21], (DEPTH, D_MODEL, D_FF), D_MODEL ** -0.5),
        'w_down': nrm(ks[22], (DEPTH, D_FF, D_MODEL), D_FF ** -0.5),
        'final_g': gain(ks[23], (D_MODEL,)),
    }


def reference(x_prompt, x_sample, c_prompt, c_sample, w_mod, b_mod, norm1_g, w_in, conv_w,
              conv_b, dt_bias, a_log, d_skip, ssd_norm_g, q_norm_g, k_norm_g, diff_lambda,
              diff_subln_g, w_out, norm2_g, w_gate, w_up, w_down, final_g):
    y_prompt = encoder(x_prompt, c_prompt, w_mod, b_mod, norm1_g, w_in, conv_w, conv_b,
                       dt_bias, a_log, d_skip, ssd_norm_g, q_norm_g, k_norm_g, diff_lambda,
                       diff_subln_g, w_out, norm2_g, w_gate, w_up, w_down, final_g)
    y_sample = encoder(x_sample, c_sample, w_mod, b_mod, norm1_g, w_in, conv_w, conv_b,
                       dt_bias, a_log, d_skip, ssd_norm_g, q_norm_g, k_norm_g, diff_lambda,
                       diff_subln_g, w_out, norm2_g, w_gate, w_up, w_down, final_g)
    return (y_prompt, y_sample)
```

```python
import math
from contextlib import ExitStack

import numpy as np

import concourse.bass as bass
import concourse.mybir as mybir
from concourse.bass_utils import run_bass_kernel_spmd

F32 = mybir.dt.float32
BF16 = mybir.dt.bfloat16
AF = mybir.ActivationFunctionType
ALU = mybir.AluOpType
AX = mybir.AxisListType

D = 2048
KD = 16
DFF = 5632
KF = 44
EPS = 1e-6
NCORES = 8
TT = 512
OFF_Z, OFF_XBC, OFF_DT, OFF_GQ, OFF_GK, OFF_GV, OFF_DQ, OFF_DK, OFF_DV = 0, 1024, 2560, 2592, 3104, 3360, 3616, 4128, 4640
NPF = 219
PF_N1, PF_N2, PF_BMOD, PF_CW, PF_CB, PF_SNG, PF_DSK, PF_QG, PF_KG, PF_SLG = 0, 16, 32, 128, 188, 200, 208, 216, 217, 218
NPB = 320
PB_DTB, PB_ALOG, PB_LAM = 0, 32, 64


class Chan:
    def __init__(self, sem, is_dma):
        self.sem = sem
        self.count = 0
        self.is_dma = is_dma


class Region:
    def __init__(self, name, chan=None):
        self.name = name
        self.w = None
        self.r = {}
        self.chan = chan


class Tile:
    def __init__(self, h, reg):
        self.h = h
        self.reg = reg


class Eng:
    def __init__(self, name, h, chan):
        self.name = name
        self.h = h
        self.chan = chan
        self.waited = {}


def _reg(x):
    return x.reg if isinstance(x, Tile) else x


class KB:
    def __init__(self, nc):
        self.nc = nc
        self.es = ExitStack()
        self.chans = []
        self.engs = {}
        for n in ("tensor", "vector", "scalar", "gpsimd", "sync"):
            self.engs[n] = Eng(n, getattr(nc, n), self.new_chan("e_" + n, False))
        self.ps_rr = 0
        self.psb = []

    def new_chan(self, name, is_dma=True):
        if not hasattr(self, "cmap"):
            self.cmap = {}
        if name in self.cmap:
            return self.cmap[name]
        sem = self.es.enter_context(self.nc.semaphore("sem_" + name))
        c = Chan(sem, is_dma)
        self.chans.append(c)
        self.cmap[name] = c
        return c

    def region(self, name, chan=None):
        return Region(name, chan)

    def dram(self, name, shape, dt, kind="Internal", chan=None):
        t = self.nc.dram_tensor(name, list(shape), dt, kind=kind).ap()
        return Tile(t, Region(name, chan))

    def sb(self, stack, name, shape, dt, chan=None):
        self.uid = getattr(self, "uid", 0) + 1
        name = f"sb_{name}_{self.uid}"
        h = stack.enter_context(self.nc.sbuf_tensor(name, list(shape), dt))
        return Tile(h, Region(name, chan))

    def psum(self, stack, name, shape, dt=F32):
        self.uid = getattr(self, "uid", 0) + 1
        name = f"pp_{name}_{self.uid}"
        h = stack.enter_context(self.nc.psum_tensor(name, list(shape), dt))
        return Tile(h, Region(name))

    def _wait(self, eng, chan, val):
        if chan.is_dma:
            val = chan.count
        if eng.waited.get(chan, 0) >= val:
            return
        eng.h.wait_ge(chan.sem, val)
        eng.waited[chan] = val

    def op(self, engname, fn, reads=(), writes=(), chan=None):
        eng = self.engs[engname]
        evs = []
        for r in reads:
            r = _reg(r)
            if r.w is not None:
                evs.append(r.w)
        for w in writes:
            w = _reg(w)
            if w.w is not None:
                evs.append(w.w)
            evs.extend(w.r.items())
        for (c, v) in evs:
            if c is eng.chan and engname == "tensor":
                continue
            self._wait(eng, c, v)
        ins = fn(eng.h)
        c = chan if chan is not None else eng.chan
        inc = 16 if c.is_dma else 1
        c.count += inc
        ins.then_inc(c.sem, inc)
        for w in writes:
            w = _reg(w)
            w.w = (c, c.count)
            w.r = {}
        for r in reads:
            r = _reg(r)
            if r.r.get(c, 0) < c.count:
                r.r[c] = c.count
        return ins

    def dma(self, q, out, in_, reads, writes, chan):
        return self.op(q, lambda e: e.dma_start(out=out, in_=in_), reads=reads, writes=writes, chan=chan)

    def barrier(self):
        for eng in self.engs.values():
            for c in self.chans:
                if c.count > 0 and eng.waited.get(c, 0) < c.count:
                    eng.h.wait_ge(c.sem, c.count)
                    eng.waited[c] = c.count

    def ps(self):
        t = self.psb[self.ps_rr % len(self.psb)]
        self.ps_rr += 1
        return t


def build_program(T, NL, dbg=None):
    dbg = dbg or {}
    SEG = T // 2
    NT = T // TT
    NCH = T // 128
    NAL = 2 * T - 128
    nc = bass.Bass("TRN2", target_bir_lowering=False)
    kb = KB(nc)
    es = kb.es

    def inp(name, shape, dt=F32):
        return kb.dram(name, shape, dt, kind="ExternalInput")

    xT_in = inp("xT", [D, T])
    cT_in = inp("cT", [128, 32])
    flags_in = inp("flags", [128, 2])
    cos_in = inp("cosT", [128, T])
    sin_in = inp("sinT", [128, T])
    alibi_in = inp("alibi", [128, NAL])
    consts_in = inp("consts", [128, 6, 128])
    pf_in = inp("pf", [128, NL * NPF + 16])
    pb_in = inp("pb", [128, NL * NPB])
    w_mod_in = inp("w_mod_t", [NL, 24, 128, KD * 512])
    w_inw_in = inp("w_inw_t", [NL, 9, 128, KD * 512])
    w_ints_in = inp("w_ints_t", [NL, 128, KD * 800])
    w_out_in = inp("w_out_t", [NL, 4, 128, KD * 512])
    w_gate_in = inp("w_gate_t", [NL, 11, 128, KD * 512])
    w_up_in = inp("w_up_t", [NL, 11, 128, KD * 512])
    w_down_in = inp("w_down_t", [NL, 16, 128, KF * 128])
    yT_out = kb.dram("yT", [D, T], F32, kind="ExternalOutput", chan=kb.new_chan("yT"))

    wsc = []
    for l in range(NL):
        d = {}
        for nm, src, shp in (("mod", w_mod_in, [24, 128, KD * 512]), ("inw", w_inw_in, [9, 128, KD * 512]),
                             ("ints", w_ints_in, [1, 128, KD * 800]), ("out", w_out_in, [4, 128, KD * 512]),
                             ("gate", w_gate_in, [11, 128, KD * 512]), ("up", w_up_in, [11, 128, KD * 512]),
                             ("down", w_down_in, [16, 128, KF * 128])):
            d[nm] = kb.dram(f"wb_{nm}{l}", shp, BF16, chan=kb.new_chan(f"wc_{nm}{l}"))
            d[nm].src = src
        wsc.append(d)

    def scr(name, shape, dt):
        return kb.dram(name, shape, dt, chan=kb.new_chan("s_" + name))

    xTs = scr("xTs", [D, T], F32)
    zT = scr("zT", [1024, T], BF16)
    xbcT = scr("xbcT", [1536, T], BF16)
    qT = scr("qT", [512, T], BF16)
    kT = scr("kT", [256, T], BF16)
    dqT = scr("dqT", [512, T], BF16)
    dkT = scr("dkT", [512, T], BF16)
    dtm = scr("dtm", [T, 32], F32)
    vg = scr("vg", [T, 256], BF16)
    vd = scr("vd", [T, 512], BF16)
    xcT = scr("xcT", [1536, T], BF16)
    Xtok = scr("Xtok", [T, 1024], BF16)
    Btok = scr("Btok", [T, 256], BF16)
    mixT = scr("mixT", [D, T], BF16)

    def emit_casts(l, names, fence=True):
        gp = kb.engs["gpsimd"]
        for nm in names:
            w = wsc[l][nm]
            nb = w.h.shape[0]
            for b in range(nb):
                src = w.src.h[l, b] if nm != "ints" else w.src.h[l]
                kb.dma("gpsimd", w.h[b], src, reads=[], writes=[w], chan=w.reg.chan)
            if fence:
                kb._wait(gp, w.reg.chan, w.reg.chan.count)

    emit_casts(0, ("mod", "inw", "ints", "out", "gate", "up", "down"))

    pst = ExitStack()
    es.enter_context(pst)
    ld = kb.new_chan("ld_const")
    consts_f = kb.sb(pst, "consts_f", [128, 6, 128], F32, chan=ld)
    pf = kb.sb(pst, "pf", [128, NL * NPF + 16], F32, chan=ld)
    pb = kb.sb(pst, "pb", [128, NL * NPB], F32, chan=ld)
    flags = kb.sb(pst, "flags", [128, 2], F32, chan=ld)
    cTs = kb.sb(pst, "cTs", [128, 32], F32, chan=ld)
    for tl, src in ((consts_f, consts_in), (pf, pf_in), (pb, pb_in), (flags, flags_in), (cTs, cT_in)):
        kb.dma("sync", tl.h[:], src.h[:], reads=[], writes=[tl], chan=ld)
    ones_bf = kb.sb(pst, "ones_bf", [128, 128], BF16)
    ones_f = kb.sb(pst, "ones_f", [128, 128], F32)
    ident_bf = kb.sb(pst, "ident_bf", [128, 128], BF16)
    rm_bf = kb.sb(pst, "rm_bf", [128, 128], BF16)
    silc = kb.sb(pst, "silc", [128, 32], BF16)
    modT = kb.sb(pst, "modT", [128, 96, 2], F32)
    modA = kb.sb(pst, "modA", [128, 2, KD, 2], F32)
    lamt = kb.sb(pst, "lamt", [128, 8], F32)
    gsl = kb.sb(pst, "gsl", [128, 1], F32)
    kb.op("vector", lambda e: e.memset(ones_bf.h[:], 1.0), writes=[ones_bf])
    kb.op("vector", lambda e: e.memset(ones_f.h[:], 1.0), writes=[ones_f])
    kb.op("vector", lambda e: e.tensor_copy(out=ident_bf.h[:], in_=consts_f.h[:, 5, :]), reads=[consts_f], writes=[ident_bf])
    kb.op("vector", lambda e: e.tensor_copy(out=rm_bf.h[:], in_=consts_f.h[:, 4, :]), reads=[consts_f], writes=[rm_bf])
    kb.op("scalar", lambda e: e.activation(out=silc.h[:], in_=cTs.h[:], func=AF.Silu), reads=[cTs], writes=[silc])
    TRI = [consts_f.h[:, 0, :], consts_f.h[:, 2, :]]
    STRICT = [consts_f.h[:, 1, :], consts_f.h[:, 3, :]]
    flag_ap = flags.h[:, 0:1]
    mask_ap = flags.h[:, 1:2]

    def new_psum(stack):
        kb.psb = [kb.psum(stack, f"ps{i}", [128, 512]) for i in range(8)]
        kb.ps_rr = 0

    def rms_bcast(stack_tiles, src_fn, nchunk, dim, srcs):
        sq, sd, rstd = stack_tiles
        pss = kb.ps()
        for k in range(nchunk):
            kb.op("scalar", lambda e, k=k: e.activation(out=sq.h[:, k, :], in_=src_fn(k), func=AF.Square),
                  reads=srcs, writes=[sq])
        def mm(e):
            ins = None
            for k in range(nchunk):
                ins = e.matmul(pss.h[:], lhsT=ones_bf.h[:], rhs=sq.h[:, k, :], start=(k == 0), stop=(k == nchunk - 1))
            return ins
        kb.op("tensor", mm, reads=[sq, ones_bf], writes=[pss])
        kb.op("scalar", lambda e: e.activation(out=sd.h[:], in_=pss.h[:], func=AF.Sqrt, bias=EPS, scale=1.0 / dim),
              reads=[pss], writes=[sd])
        kb.op("vector", lambda e: e.reciprocal(out=rstd.h[:], in_=sd.h[:]), reads=[sd], writes=[rstd])
        return rstd

    wrr = [0]

    def load_w(wbufs, wt, b, ncols):
        buf = wbufs[wrr[0] % len(wbufs)]
        wrr[0] += 1
        kb.dma("sync", buf.h[:, 0:ncols], wt.h[b], reads=[wt], writes=[buf], chan=buf.reg.chan)
        return buf

    def norm_mod(xt, actT, sqt, a_fn, b_fn, tmp):
        rstd = rms_bcast(sqt, lambda k: xt.h[:, k, :], KD, D, [xt])
        for k in range(KD):
            tb = tmp[k % 2]
            kb.op("vector", lambda e, k=k, tb=tb: e.scalar_tensor_tensor(out=tb.h[:], in0=xt.h[:, k, :], scalar=a_fn(k),
                                                                        in1=rstd.h[:], op0=ALU.mult, op1=ALU.mult),
                  reads=[xt, rstd, modA, modT], writes=[tb])
            kb.op("scalar", lambda e, k=k, tb=tb: e.activation(out=actT.h[:, k, :], in_=tb.h[:], func=AF.Identity,
                                                              bias=b_fn(k), scale=1.0),
                  reads=[tb, modT], writes=[actT])


    def zero_mix(c0, c1):
        with ExitStack() as st:
            zt = kb.sb(st, "zz", [128, c1 - c0, TT], BF16)
            kb.op("vector", lambda e: e.memset(zt.h[:], 0.0), writes=[zt])
            mv = mixT.h.rearrange("(k p) t -> p k t", p=128)
            for t in range(NT):
                kb.dma("sync", mv[:, c0:c1, t * TT:(t + 1) * TT], zt.h[:], reads=[zt], writes=[mixT], chan=mixT.reg.chan)
            kb.barrier()

    OFFA = T - 128

    def emit_attn(l, pfo):
        with ExitStack() as st:
            acc = [kb.psum(st, f"acc{i}", [128, 512]) for i in range(4)]
            scb = [kb.psum(st, f"scb{i}", [128, 512]) for i in range(4)]
            kt = kb.sb(st, "c_kt", [128, T], BF16, chan=kb.new_chan("c_kt"))
            vt = kb.sb(st, "c_vt", [128, NCH, 128], BF16, chan=kb.new_chan("c_vt"))
            qts = [kb.sb(st, f"c_qt{i}", [128, TT], BF16, chan=kb.new_chan(f"c_qt{i}")) for i in range(2)]
            pTs = [kb.sb(st, f"c_p{i}", [128, TT], BF16) for i in range(6)]
            sbs = [kb.sb(st, f"c_sb{i}", [128, TT], F32) for i in range(6)]
            alb = kb.sb(st, "c_alb", [128, NAL], F32, chan=kb.new_chan("c_alb"))
            rr = [kb.sb(st, f"c_r{i}", [128, TT], F32) for i in range(2)]
            tt_ = [kb.sb(st, f"c_t{i}", [128, TT], F32) for i in range(2)]
            osb = kb.sb(st, "c_o", [128, TT], F32)
            osq = kb.sb(st, "c_osq", [128, TT], BF16)
            osd = kb.sb(st, "c_osd", [128, TT], F32)
            ors = kb.sb(st, "c_ors", [128, TT], F32)
            ostg = [kb.sb(st, f"c_os{i}", [128, TT], BF16) for i in range(2)]
            kb.dma("sync", alb.h[:], alibi_in.h[:], reads=[], writes=[alb], chan=alb.reg.chan)
            cnt = [0, 0, 0, 0]

            def core(qt, r0, r1, slope, scale, t, o_ps, s_ps):
                banks = {}

                def qk2(cp):
                    b0 = scb[2 * (cnt[0] % 2)]
                    b1 = scb[2 * (cnt[0] % 2) + 1]
                    cnt[0] += 1
                    banks[2 * cp] = b0
                    banks[2 * cp + 1] = b1
                    def mm(e, cp=cp, b0=b0, b1=b1):
                        e.matmul(b0.h[:], lhsT=kt.h[r0:r1, (2 * cp) * 128:(2 * cp + 1) * 128], rhs=qt.h[r0:r1, :], start=True, stop=True)
                        return e.matmul(b1.h[:], lhsT=kt.h[r0:r1, (2 * cp + 1) * 128:(2 * cp + 2) * 128], rhs=qt.h[r0:r1, :], start=True, stop=True)
                    kb.op("tensor", mm, reads=[kt, qt], writes=[b0, b1])
                NP = NCH // 2
                qk2(0)
                if NP > 1:
                    qk2(1)
                for cp in range(NP):
                    ps_ = []
                    for c in (2 * cp, 2 * cp + 1):
                        sc_ps = banks.pop(c)
                        cross = (c // (NCH // 2)) != (t // (NT // 2))
                        bias = mask_ap if cross else 0.0
                        if slope is not None:
                            sbt = sbs[cnt[1] % 6]
                            cnt[1] += 1
                            off = t * TT - c * 128 + OFFA
                            kb.op("vector", lambda e, sbt=sbt, off=off, sc_ps=sc_ps: e.scalar_tensor_tensor(
                                out=sbt.h[:], in0=alb.h[:, off:off + TT], scalar=slope / scale, in1=sc_ps.h[:], op0=ALU.mult, op1=ALU.add),
                                reads=[alb, sc_ps], writes=[sbt])
                            src = sbt
                        else:
                            src = sc_ps
                        p = pTs[cnt[2] % 6]
                        cnt[2] += 1
                        ps_.append(p)
                        kb.op("scalar", lambda e, p=p, src=src, bias=bias: e.activation(out=p.h[:], in_=src.h[:], func=AF.Exp, bias=bias, scale=scale),
                              reads=[src, flags], writes=[p])
                    if cp + 2 < NP:
                        qk2(cp + 2)
                    def mm2(e, cp=cp, ps_=ps_):
                        ins = None
                        for q, c in enumerate((2 * cp, 2 * cp + 1)):
                            e.matmul(o_ps.h[:], lhsT=vt.h[:, c, :], rhs=ps_[q].h[:], start=(c == 0), stop=(c == NCH - 1))
                            ins = e.matmul(s_ps.h[:], lhsT=ones_bf.h[:], rhs=ps_[q].h[:], start=(c == 0), stop=(c == NCH - 1))
                        return ins
                    kb.op("tensor", mm2, reads=[vt, ps_[0], ps_[1], ones_bf], writes=[o_ps, s_ps])

            mv = mixT.h.rearrange("(k p) t -> p k t", p=128)
            qi = 0
            for g in range(2):
                kb.dma("sync", kt.h[:], kT.h[g * 128:(g + 1) * 128, :], reads=[kT], writes=[kt], chan=kt.reg.chan)
                kb.dma("sync", vt.h[:], vg.h[:, g * 128:(g + 1) * 128].rearrange("(c p) d -> p c d", p=128), reads=[vg], writes=[vt], chan=vt.reg.chan)
                for hq in range(2):
                    h = 2 * g + hq
                    for t in range(NT):
                        ts_ = slice(t * TT, (t + 1) * TT)
                        qt = qts[qi % 2]
                        qi += 1
                        kb.dma("sync", qt.h[:], qT.h[h * 128:(h + 1) * 128, ts_], reads=[qT], writes=[qt], chan=qt.reg.chan)
                        o_ps, s_ps = acc[2 * (qi % 2)], acc[2 * (qi % 2) + 1]
                        core(qt, 0, 128, None, 128 ** -0.5, t, o_ps, s_ps)
                        r = rr[qi % 2]
                        og = ostg[qi % 2]
                        kb.op("vector", lambda e, r=r, s_ps=s_ps: e.reciprocal(out=r.h[:], in_=s_ps.h[:]), reads=[s_ps], writes=[r])
                        kb.op("vector", lambda e, r=r, o_ps=o_ps, og=og: e.tensor_tensor(out=og.h[:], in0=o_ps.h[:], in1=r.h[:], op=ALU.mult),
                              reads=[o_ps, r], writes=[og])
                        kb.dma("sync", mv[:, 8 + h, ts_], og.h[:], reads=[og], writes=[mixT], chan=mixT.reg.chan)
            for h in range(4):
                slope = 2.0 ** (-2.0 * (h + 1))
                kb.dma("sync", kt.h[:], dkT.h[h * 128:(h + 1) * 128, :], reads=[dkT], writes=[kt], chan=kt.reg.chan)
                kb.dma("sync", vt.h[:], vd.h[:, h * 128:(h + 1) * 128].rearrange("(c p) d -> p c d", p=128), reads=[vd], writes=[vt], chan=vt.reg.chan)
                for t in range(NT):
                    ts_ = slice(t * TT, (t + 1) * TT)
                    qt = qts[qi % 2]
                    qi += 1
                    kb.dma("sync", qt.h[:], dqT.h[h * 128:(h + 1) * 128, ts_], reads=[dqT], writes=[qt], chan=qt.reg.chan)
                    core(qt, 0, 64, slope, 0.125, t, acc[0], acc[1])
                    core(qt, 64, 128, slope, 0.125, t, acc[2], acc[3])
                    for i in range(2):
                        kb.op("vector", lambda e, i=i: e.reciprocal(out=rr[i].h[:], in_=acc[2 * i + 1].h[:]), reads=[acc[2 * i + 1]], writes=[rr[i]])
                        kb.op("vector", lambda e, i=i: e.tensor_tensor(out=tt_[i].h[:], in0=acc[2 * i].h[:], in1=rr[i].h[:], op=ALU.mult),
                              reads=[acc[2 * i], rr[i]], writes=[tt_[i]])
                    kb.op("vector", lambda e: e.scalar_tensor_tensor(out=osb.h[:], in0=tt_[1].h[:], scalar=lamt.h[:, 0:1], in1=tt_[0].h[:],
                                                                    op0=ALU.mult, op1=ALU.add), reads=[tt_[0], tt_[1], lamt], writes=[osb])
                    kb.op("scalar", lambda e: e.activation(out=osq.h[:], in_=osb.h[:], func=AF.Square), reads=[osb], writes=[osq])
                    nps = scb[2 * (cnt[0] % 2)]
                    cnt[0] += 1
                    kb.op("tensor", lambda e, nps=nps: e.matmul(nps.h[:], lhsT=ones_bf.h[:], rhs=osq.h[:], start=True, stop=True), reads=[osq, ones_bf], writes=[nps])
                    kb.op("scalar", lambda e, nps=nps: e.activation(out=osd.h[:], in_=nps.h[:], func=AF.Sqrt, bias=EPS, scale=1.0 / 128), reads=[nps], writes=[osd])
                    kb.op("vector", lambda e: e.reciprocal(out=ors.h[:], in_=osd.h[:]), reads=[osd], writes=[ors])
                    og = ostg[qi % 2]
                    kb.op("vector", lambda e, og=og: e.scalar_tensor_tensor(out=og.h[:], in0=osb.h[:], scalar=gsl.h[:, 0:1], in1=ors.h[:],
                                                                          op0=ALU.mult, op1=ALU.mult), reads=[osb, gsl, ors], writes=[og])
                    kb.dma("sync", mv[:, 12 + h, ts_], og.h[:], reads=[og], writes=[mixT], chan=mixT.reg.chan)
            kb.barrier()

    def emit_ssd(l, pfo, pbo):
        with ExitStack() as st:
            ptx = kb.psum(st, "ptx", [128, 1024], BF16)
            ptb = kb.psum(st, "ptb", [128, 256], BF16)
            xp = kb.sb(st, "b_xp", [128, 12, TT + 4], BF16, chan=kb.new_chan("b_xp"))
            acc = kb.sb(st, "b_acc", [128, 12, TT], F32)
            xc = kb.sb(st, "b_xc", [128, 12, TT], BF16)
            sX = kb.sb(st, "b_sX", [128, 4, 1024], BF16)
            sB = kb.sb(st, "b_sB", [128, 4, 256], BF16)
            xv = xbcT.h.rearrange("(c p) t -> p c t", p=128)
            cw = lambda k, c: pf.h[:, pfo + PF_CW + k * 12 + c:pfo + PF_CW + k * 12 + c + 1]
            cb = lambda c: pf.h[:, pfo + PF_CB + c:pfo + PF_CB + c + 1]
            for t in range(NT):
                lo = max(t * TT - 2, 0)
                hi = min(t * TT + TT + 2, T)
                d0 = lo - (t * TT - 2)
                kb.dma("sync", xp.h[:, :, d0:d0 + (hi - lo)], xv[:, :, lo:hi], reads=[xbcT], writes=[xp], chan=xp.reg.chan)
                if t == 0:
                    kb.op("vector", lambda e: e.memset(xp.h[:, :, 0:2], 0.0), writes=[xp])
                if t == NT - 1:
                    kb.op("vector", lambda e: e.memset(xp.h[:, :, TT + 2:TT + 4], 0.0), writes=[xp])
                if t == NT // 2:
                    kb.op("vector", lambda e: e.tensor_scalar(out=xp.h[:, :, 0:2], in0=xp.h[:, :, 0:2], scalar1=flag_ap, scalar2=None, op0=ALU.mult),
                          reads=[xp, flags], writes=[xp])
                if t == NT // 2 - 1:
                    kb.op("vector", lambda e: e.tensor_scalar(out=xp.h[:, :, TT + 2:TT + 4], in0=xp.h[:, :, TT + 2:TT + 4], scalar1=flag_ap, scalar2=None, op0=ALU.mult),
                          reads=[xp, flags], writes=[xp])
                for c in range(12):
                    kb.op("vector", lambda e, c=c: e.tensor_scalar(out=acc.h[:, c, :], in0=xp.h[:, c, 0:TT], scalar1=cw(0, c), scalar2=cb(c),
                                                                  op0=ALU.mult, op1=ALU.add), reads=[xp, pf], writes=[acc])
                    for k in range(1, 5):
                        kb.op("vector", lambda e, c=c, k=k: e.scalar_tensor_tensor(out=acc.h[:, c, :], in0=xp.h[:, c, k:k + TT], scalar=cw(k, c),
                                                                                  in1=acc.h[:, c, :], op0=ALU.mult, op1=ALU.add),
                              reads=[xp, pf, acc], writes=[acc])
                kb.op("scalar", lambda e: e.activation(out=xc.h[:], in_=acc.h[:], func=AF.Silu), reads=[acc], writes=[xc])
                kb.dma("sync", xcT.h.rearrange("(c p) t -> p c t", p=128)[:, :, t * TT:(t + 1) * TT], xc.h[:], reads=[xc], writes=[xcT], chan=xcT.reg.chan)
                for tb in range(4):
                    def tr(e, tb=tb):
                        ins = None
                        for c in range(8):
                            ins = e.transpose(ptx.h[:, c * 128:(c + 1) * 128], xc.h[:, c, tb * 128:(tb + 1) * 128], ident_bf.h[:])
                        for c in range(2):
                            ins = e.transpose(ptb.h[:, c * 128:(c + 1) * 128], xc.h[:, 8 + c, tb * 128:(tb + 1) * 128], ident_bf.h[:])
                        return ins
                    kb.op("tensor", tr, reads=[xc, ident_bf], writes=[ptx, ptb])
                    kb.op("vector", lambda e, tb=tb: e.tensor_copy(out=sX.h[:, tb, :], in_=ptx.h[:]), reads=[ptx], writes=[sX])
                    kb.op("scalar", lambda e, tb=tb: e.copy(out=sB.h[:, tb, :], in_=ptb.h[:]), reads=[ptb], writes=[sB])
                kb.dma("sync", Xtok.h[t * TT:(t + 1) * TT, :].rearrange("(b p) f -> p b f", p=128), sX.h[:], reads=[sX], writes=[Xtok], chan=Xtok.reg.chan)
                kb.dma("sync", Btok.h[t * TT:(t + 1) * TT, :].rearrange("(b p) f -> p b f", p=128), sB.h[:], reads=[sB], writes=[Btok], chan=Btok.reg.chan)
            kb.barrier()
        sto = ExitStack()
        ytot = kb.sb(sto, "s_ytot", [128, NCH, 1024], BF16)
        with ExitStack() as st:
            ps2 = [kb.psum(st, f"ps2{i}", [128, 1024]) for i in range(2)]
            psy = kb.psum(st, "psy", [128, 512])
            pso = kb.psum(st, "pso", [128, 512])
            pss_ = kb.psum(st, "pst", [128, 512])
            psm = kb.psum(st, "psm", [128, 512])
            state = [[kb.sb(st, f"s_st{d}{g}", [128, 8, 64], F32) for g in range(2)] for d in range(2)]
            prevb = [[kb.sb(st, f"s_pv{d}{g}", [128, 512], BF16) for g in range(2)] for d in range(2)]
            ab = kb.sb(st, "s_ab", [128, 32], F32)
            ldc = [kb.new_chan(f"s_ld{i}") for i in range(2)]
            Xc = [kb.sb(st, f"s_X{i}", [128, 16, 64], BF16, chan=ldc[i]) for i in range(2)]
            Bc = [kb.sb(st, f"s_B{i}", [128, 256], BF16, chan=ldc[i]) for i in range(2)]
            BCT = [kb.sb(st, f"s_BCT{i}", [128, 4, 128], BF16, chan=ldc[i]) for i in range(2)]
            dtc = [kb.sb(st, f"s_dt{i}", [128, 32], F32, chan=ldc[i]) for i in range(2)]
            u_ = kb.sb(st, "s_u", [128, 32], F32)
            ax_ = kb.sb(st, "s_ax", [128, 32], F32)
            ee_ = kb.sb(st, "s_ee", [128, 32], F32)
            ll_ = kb.sb(st, "s_ll", [128, 32], F32)
            dtp = [kb.sb(st, f"s_dtp{i}", [128, 32], F32) for i in range(2)]
            adt = [kb.sb(st, f"s_adt{i}", [128, 32], F32) for i in range(2)]
            rhsA = [kb.sb(st, f"s_rA{i}", [128, 8, 128], F32) for i in range(2)]
            dec = [kb.sb(st, f"s_dec{i}", [128, 8, 128], BF16) for i in range(2)]
            msc = [kb.sb(st, f"s_msc{i}", [128, 128], BF16) for i in range(2)]
            MT = [kb.sb(st, f"s_MT{i}", [128, 8, 128], BF16) for i in range(2)]
            csb = [kb.sb(st, f"s_csb{i}", [128, 16], F32) for i in range(2)]
            ecs = [kb.sb(st, f"s_ecs{i}", [128, 8], F32) for i in range(2)]
            tmc = [kb.sb(st, f"s_tmc{i}", [128, 8], F32) for i in range(2)]
            dte = [kb.sb(st, f"s_dte{i}", [128, 8], F32) for i in range(2)]
            cd = [kb.sb(st, f"s_cd{i}", [128, 8], F32) for i in range(2)]
            xdt = [kb.sb(st, f"s_xdt{i}", [128, 8, 64], BF16) for i in range(2)]
            xdte = [kb.sb(st, f"s_xdte{i}", [128, 8, 64], BF16) for i in range(2)]
            yo = [kb.sb(st, f"s_yo{i}", [128, 8, 64], F32) for i in range(2)]
            yc = [kb.sb(st, f"s_yc{i}", [128, 8, 64], F32) for i in range(2)]
            st1 = [kb.sb(st, f"s_st1{i}", [128, 8, 64], F32) for i in range(2)]
            kb.op("scalar", lambda e: e.activation(out=ab.h[:], in_=pb.h[:, pbo + PB_ALOG:pbo + PB_ALOG + 32], func=AF.Exp), reads=[pb], writes=[ab])
            kb.op("vector", lambda e: e.tensor_scalar(out=ab.h[:], in0=ab.h[:], scalar1=-1.0, scalar2=None, op0=ALU.mult), reads=[ab], writes=[ab])
            for d in range(2):
                for g in range(2):
                    kb.op("vector", lambda e, d=d, g=g: e.memset(state[d][g].h[:], 0.0), writes=[state[d][g]])
                    kb.op("vector", lambda e, d=d, g=g: e.memset(prevb[d][g].h[:], 0.0), writes=[prevb[d][g]])

            def pro(d, c, i):
                cs_ = slice(c * 128, (c + 1) * 128)
                kb.dma("sync", Xc[i].h[:].rearrange("p h f -> p (h f)"), Xtok.h[cs_, :], reads=[Xtok], writes=[Xc[i]], chan=ldc[i])
                kb.dma("sync", Bc[i].h[:], Btok.h[cs_, :], reads=[Btok], writes=[Bc[i]], chan=ldc[i])
                kb.dma("sync", BCT[i].h[:], xcT.h[1024:1536, cs_].rearrange("(j p) t -> p j t", p=128), reads=[xcT], writes=[BCT[i]], chan=ldc[i])
                kb.dma("sync", dtc[i].h[:], dtm.h[cs_, :], reads=[dtm], writes=[dtc[i]], chan=ldc[i])
                kb.op("vector", lambda e: e.tensor_tensor(out=u_.h[:], in0=dtc[i].h[:], in1=pb.h[:, pbo + PB_DTB:pbo + PB_DTB + 32], op=ALU.add),
                      reads=[dtc[i], pb], writes=[u_])
                kb.op("vector", lambda e: e.tensor_scalar(out=ax_.h[:], in0=u_.h[:], scalar1=-1.0, scalar2=None, op0=ALU.mult), reads=[u_], writes=[ax_])
                kb.op("vector", lambda e: e.tensor_tensor(out=ax_.h[:], in0=ax_.h[:], in1=u_.h[:], op=ALU.min), reads=[u_, ax_], writes=[ax_])
                kb.op("scalar", lambda e: e.activation(out=ee_.h[:], in_=ax_.h[:], func=AF.Exp), reads=[ax_], writes=[ee_])
                kb.op("scalar", lambda e: e.activation(out=ll_.h[:], in_=ee_.h[:], func=AF.Ln, bias=1.0, scale=1.0), reads=[ee_], writes=[ll_])
                kb.op("vector", lambda e: e.scalar_tensor_tensor(out=dtp[i].h[:], in0=u_.h[:], scalar=0.0, in1=ll_.h[:], op0=ALU.max, op1=ALU.add),
                      reads=[u_, ll_], writes=[dtp[i]])
                kb.op("vector", lambda e: e.tensor_tensor(out=adt[i].h[:], in0=dtp[i].h[:], in1=ab.h[:], op=ALU.mult), reads=[dtp[i], ab], writes=[adt[i]])

            PE_ = dbg.get("pool_eng", "gpsimd")

            def front(d, c, g, i, j):
                hs = d * 16 + g * 8
                p2 = ps2[j]
                kb.op(PE_, lambda e: e.tensor_tensor(
                    out=rhsA[j].h[:], in0=TRI[d].unsqueeze(1).broadcast_to([128, 8, 128]),
                    in1=adt[i].h[:, hs:hs + 8].unsqueeze(2).broadcast_to([128, 8, 128]), op=ALU.mult),
                    reads=[consts_f, adt[i]], writes=[rhsA[j]])
                def mmA(e):
                    rv = rhsA[j].h[:].rearrange("p h l -> p (h l)")
                    e.matmul(p2.h[:, 0:512], lhsT=STRICT[d], rhs=rv[:, 0:512], start=True, stop=True)
                    return e.matmul(p2.h[:, 512:1024], lhsT=STRICT[d], rhs=rv[:, 512:1024], start=True, stop=True)
                kb.op("tensor", mmA, reads=[rhsA[j], consts_f], writes=[p2])
                def mmB(e):
                    e.matmul(psm.h[:, 0:8], lhsT=TRI[d], rhs=adt[i].h[:, hs:hs + 8], start=True, stop=True)
                    e.matmul(psm.h[:, 8:16], lhsT=ones_f.h[:], rhs=adt[i].h[:, hs:hs + 8], start=True, stop=True)
                    return e.matmul(psm.h[:, 128:256], lhsT=BCT[i].h[:, g, :], rhs=BCT[i].h[:, 2 + g, :], start=True, stop=True)
                kb.op("tensor", mmB, reads=[adt[i], consts_f, ones_f, BCT[i]], writes=[psm])
                kb.op("scalar", lambda e: e.activation(out=dec[j].h[:].rearrange("p h l -> p (h l)"), in_=p2.h[:], func=AF.Exp),
                      reads=[p2], writes=[dec[j]])
                kb.op(PE_, lambda e: e.tensor_tensor(
                    out=xdt[j].h[:], in0=Xc[i].h[:, g * 8:(g + 1) * 8, :],
                    in1=dtp[i].h[:, hs:hs + 8].unsqueeze(2).broadcast_to([128, 8, 64]), op=ALU.mult),
                    reads=[Xc[i], dtp[i]], writes=[xdt[j]])

            def front2(d, c, g, i, j):
                hs = d * 16 + g * 8
                kb.op("vector", lambda e: e.tensor_tensor(out=msc[j].h[:], in0=psm.h[:, 128:256], in1=TRI[d], op=ALU.mult),
                      reads=[psm, consts_f], writes=[msc[j]])
                kb.op("vector", lambda e: e.tensor_copy(out=csb[j].h[:], in_=psm.h[:, 0:16]), reads=[psm], writes=[csb[j]])
                kb.op("vector", lambda e: e.tensor_tensor(out=MT[j].h[:], in0=dec[j].h[:],
                                                         in1=msc[j].h[:].unsqueeze(1).broadcast_to([128, 8, 128]), op=ALU.mult),
                      reads=[dec[j], msc[j]], writes=[MT[j]])
                kb.op("scalar", lambda e: e.activation(out=ecs[j].h[:], in_=csb[j].h[:, 0:8], func=AF.Exp), reads=[csb[j]], writes=[ecs[j]])
                kb.op("vector", lambda e: e.tensor_tensor(out=tmc[j].h[:], in0=csb[j].h[:, 8:16], in1=csb[j].h[:, 0:8], op=ALU.subtract),
                      reads=[csb[j]], writes=[tmc[j]])
                kb.op("scalar", lambda e: e.activation(out=dte[j].h[:], in_=tmc[j].h[:], func=AF.Exp), reads=[tmc[j]], writes=[dte[j]])
                kb.op("scalar", lambda e: e.activation(out=cd[j].h[:], in_=csb[j].h[:, 8:16], func=AF.Exp), reads=[csb[j]], writes=[cd[j]])
                kb.op(PE_, lambda e: e.tensor_tensor(out=xdte[j].h[:], in0=xdt[j].h[:],
                                                         in1=dte[j].h[:].unsqueeze(2).broadcast_to([128, 8, 64]), op=ALU.mult),
                      reads=[xdt[j], dte[j]], writes=[xdte[j]])

            def back(d, c, g, i, j):
                def mmY(e):
                    ins = None
                    for h in range(8):
                        ins = e.matmul(psy.h[:, h * 64:(h + 1) * 64], lhsT=MT[j].h[:, h, :], rhs=xdt[j].h[:, h, :], start=True, stop=True)
                    return ins
                kb.op("tensor", mmY, reads=[MT[j], xdt[j]], writes=[psy])
                kb.op("tensor", lambda e: e.matmul(pso.h[:], lhsT=BCT[i].h[:, 2 + g, :], rhs=prevb[d][g].h[:], start=True, stop=True),
                      reads=[BCT[i], prevb[d][g]], writes=[pso])
                kb.op("tensor", lambda e: e.matmul(pss_.h[:], lhsT=Bc[i].h[:, g * 128:(g + 1) * 128],
                                                  rhs=xdte[j].h[:].rearrange("p h f -> p (h f)"), start=True, stop=True),
                      reads=[Bc[i], xdte[j]], writes=[pss_])

            def back_dve(d, c, g, i, j):
                kb.op("vector", lambda e: e.tensor_tensor(out=yo[j].h[:], in0=pso.h[:].rearrange("p (h f) -> p h f", h=8),
                                                         in1=ecs[j].h[:].unsqueeze(2).broadcast_to([128, 8, 64]), op=ALU.mult),
                      reads=[pso, ecs[j]], writes=[yo[j]])
                yslot = ytot.h[:, c, g * 512:(g + 1) * 512].rearrange("p (h f) -> p h f", h=8)
                if d == 0:
                    kb.op("vector", lambda e: e.tensor_tensor(out=yslot, in0=yo[j].h[:],
                                                             in1=psy.h[:].rearrange("p (h f) -> p h f", h=8), op=ALU.add),
                          reads=[yo[j], psy], writes=[ytot])
                else:
                    kb.op("vector", lambda e: e.tensor_tensor(out=yc[j].h[:], in0=yo[j].h[:],
                                                             in1=psy.h[:].rearrange("p (h f) -> p h f", h=8), op=ALU.add),
                          reads=[yo[j], psy], writes=[yc[j]])
                    kb.op("vector", lambda e: e.tensor_tensor(out=yslot, in0=yc[j].h[:], in1=yslot, op=ALU.add),
                          reads=[yc[j], ytot], writes=[ytot])
                sg_ = state[d][g]
                kb.op("vector", lambda e: e.tensor_tensor(out=st1[j].h[:], in0=sg_.h[:],
                                                         in1=cd[j].h[:].unsqueeze(2).broadcast_to([128, 8, 64]), op=ALU.mult),
                      reads=[sg_, cd[j]], writes=[st1[j]])
                kb.op("vector", lambda e: e.tensor_tensor(out=sg_.h[:], in0=st1[j].h[:],
                                                         in1=pss_.h[:].rearrange("p (h f) -> p h f", h=8), op=ALU.add),
                      reads=[st1[j], pss_], writes=[sg_])
                bnd = (NCH // 2 - 1) if d == 0 else (NCH // 2)
                if c == bnd:
                    kb.op("vector", lambda e: e.tensor_scalar(out=sg_.h[:], in0=sg_.h[:], scalar1=flag_ap, scalar2=None, op0=ALU.mult),
                          reads=[sg_, flags], writes=[sg_])
                kb.op("vector", lambda e: e.tensor_copy(out=prevb[d][g].h[:].rearrange("p (h f) -> p h f", h=8), in_=sg_.h[:]),
                      reads=[sg_], writes=[prevb[d][g]])

            units = []
            vi = 0
            for d in range(2):
                chunks = range(NCH) if d == 0 else range(NCH - 1, -1, -1)
                for c in chunks:
                    for g in range(2):
                        units.append((d, c, g, vi % 2, len(units) % 2, g == 0))
                    vi += 1

            def emit_front1(u):
                d, c, g, i, j, first = u
                if first:
                    pro(d, c, i)
                front(d, c, g, i, j)
            emit_front1(units[0])
            front2(*units[0][:5])
            for n, u in enumerate(units):
                back(*u[:5])
                if n + 1 < len(units):
                    emit_front1(units[n + 1])
                back_dve(*u[:5])
                if n + 1 < len(units):
                    front2(*units[n + 1][:5])
            kb.barrier()
        with ExitStack() as st:
            ptT = [kb.psum(st, f"ptT{i}", [128, 512], BF16) for i in range(2)]
            psy = kb.psum(st, "psy2", [128, 512])
            zt = kb.sb(st, "p_zt", [128, 8, TT], BF16, chan=kb.new_chan("p_zt"))
            xx = kb.sb(st, "p_xx", [128, 8, TT], BF16, chan=zt.reg.chan)
            uu = [kb.sb(st, f"p_uu{i}", [128, TT], F32) for i in range(2)]
            sz = [kb.sb(st, f"p_sz{i}", [128, TT], F32) for i in range(2)]
            ug = kb.sb(st, "p_ug", [128, 8, TT], F32)
            sq = kb.sb(st, "p_sq", [128, 8, TT], BF16)
            sd = kb.sb(st, "p_sd", [128, TT], F32)
            rs = kb.sb(st, "p_rs", [128, TT], F32)
            og = kb.sb(st, "p_og", [128, 8, TT], BF16)
            dsk = lambda fc: pf.h[:, pfo + PF_DSK + fc:pfo + PF_DSK + fc + 1]
            sng = lambda fc: pf.h[:, pfo + PF_SNG + fc:pfo + PF_SNG + fc + 1]
            mv = mixT.h.rearrange("(k p) t -> p k t", p=128)
            for t in range(NT):
                ts_ = slice(t * TT, (t + 1) * TT)
                kb.dma("sync", zt.h[:], zT.h.rearrange("(c p) t -> p c t", p=128)[:, :, ts_], reads=[zT], writes=[zt], chan=zt.reg.chan)
                kb.dma("sync", xx.h[:], xcT.h[0:1024, :].rearrange("(c p) t -> p c t", p=128)[:, :, ts_], reads=[xcT], writes=[xx], chan=zt.reg.chan)
                for fc in range(8):
                    pT = ptT[fc % 2]
                    def tr(e, fc=fc, pT=pT):
                        ins = None
                        for tb in range(4):
                            ins = e.transpose(pT.h[:, tb * 128:(tb + 1) * 128], ytot.h[:, t * 4 + tb, fc * 128:(fc + 1) * 128], ident_bf.h[:])
                        return ins
                    kb.op("tensor", tr, reads=[ytot, ident_bf], writes=[pT])
                    u = uu[fc % 2]
                    s_ = sz[fc % 2]
                    kb.op("vector", lambda e, fc=fc, pT=pT, u=u: e.scalar_tensor_tensor(out=u.h[:], in0=xx.h[:, fc, :], scalar=dsk(fc), in1=pT.h[:],
                                                                                      op0=ALU.mult, op1=ALU.add), reads=[xx, pf, pT], writes=[u])
                    kb.op("scalar", lambda e, fc=fc, s_=s_: e.activation(out=s_.h[:], in_=zt.h[:, fc, :], func=AF.Silu), reads=[zt], writes=[s_])
                    kb.op("vector", lambda e, fc=fc, u=u, s_=s_: e.tensor_tensor(out=ug.h[:, fc, :], in0=u.h[:], in1=s_.h[:], op=ALU.mult),
                          reads=[u, s_], writes=[ug])
                    kb.op("scalar", lambda e, fc=fc: e.activation(out=sq.h[:, fc, :], in_=ug.h[:, fc, :], func=AF.Square), reads=[ug], writes=[sq])
                for g in range(2):
                    def mmN(e, g=g):
                        ins = None
                        for q in range(4):
                            ins = e.matmul(psy.h[:], lhsT=ones_bf.h[:], rhs=sq.h[:, g * 4 + q, :], start=(q == 0), stop=(q == 3))
                        return ins
                    kb.op("tensor", mmN, reads=[sq, ones_bf], writes=[psy])
                    kb.op("scalar", lambda e: e.activation(out=sd.h[:], in_=psy.h[:], func=AF.Sqrt, bias=EPS, scale=1.0 / 512), reads=[psy], writes=[sd])
                    kb.op("vector", lambda e: e.reciprocal(out=rs.h[:], in_=sd.h[:]), reads=[sd], writes=[rs])
                    for q in range(4):
                        fc = g * 4 + q
                        kb.op("vector", lambda e, fc=fc: e.scalar_tensor_tensor(out=og.h[:, fc, :], in0=ug.h[:, fc, :], scalar=sng(fc), in1=rs.h[:],
                                                                               op0=ALU.mult, op1=ALU.mult), reads=[ug, pf, rs], writes=[og])
                kb.dma("sync", mv[:, 0:8, ts_], og.h[:], reads=[og], writes=[mixT], chan=mixT.reg.chan)
            kb.barrier()
        sto.close()

    for l in range(NL):
        W = wsc[l]
        pfo = l * NPF
        pbo = l * NPB
        xsrc = xT_in if l == 0 else xTs
        lam_init = 0.8 - 0.6 * math.exp(-0.3 * l)

        with ExitStack() as st:
            new_psum(st)
            wbufs = [kb.sb(st, f"mw{i}", [128, KD * 512], BF16, chan=kb.new_chan(f"mw{i}")) for i in range(2)]
            psM = kb.ps()
            for b in range(24):
                buf = load_w(wbufs, W["mod"], b, KD * 512)
                wv = buf.h[:].rearrange("p (k c) -> p k c", k=KD)
                def mm(e, b=b, wv=wv):
                    ins = None
                    for cc in range(4):
                        j = b * 4 + cc
                        for k in range(KD):
                            ins = e.matmul(psM.h[:, 2 * j:2 * j + 2], lhsT=wv[:, k, cc * 128:(cc + 1) * 128],
                                           rhs=silc.h[:, 2 * k:2 * k + 2], start=(k == 0), stop=(k == KD - 1))
                    return ins
                kb.op("tensor", mm, reads=[buf, silc], writes=[psM])
            kb.op("vector", lambda e: e.tensor_tensor(
                out=modT.h[:], in0=psM.h[:, 0:192].rearrange("p (j s) -> p j s", s=2),
                in1=pf.h[:, pfo + PF_BMOD:pfo + PF_BMOD + 96].unsqueeze(2).broadcast_to([128, 96, 2]), op=ALU.add),
                reads=[psM, pf], writes=[modT])
            for i, (sc0, g0) in enumerate(((16, PF_N1), (64, PF_N2))):
                kb.op("vector", lambda e, i=i, sc0=sc0, g0=g0: e.scalar_tensor_tensor(
                    out=modA.h[:, i], in0=modT.h[:, sc0:sc0 + 16, :], scalar=1.0,
                    in1=pf.h[:, pfo + g0:pfo + g0 + 16].unsqueeze(2).broadcast_to([128, 16, 2]),
                    op0=ALU.add, op1=ALU.mult), reads=[modT, pf], writes=[modA])
            lw = kb.sb(st, "lw", [128, 128], F32)
            lp = pb.h[:, pbo + PB_LAM:pbo + PB_LAM + 256]
            kb.op("vector", lambda e: e.tensor_tensor(out=lw.h[:].rearrange("p (a b) -> p a b", a=2),
                                                     in0=lp.rearrange("p (a t b) -> p a t b", a=2, t=2)[:, :, 0, :],
                                                     in1=lp.rearrange("p (a t b) -> p a t b", a=2, t=2)[:, :, 1, :], op=ALU.mult),
                  reads=[pb], writes=[lw])
            kb.op("vector", lambda e: e.reduce_sum(out=lamt.h[:, 2:4], in_=lw.h[:].rearrange("p (a b) -> p a b", a=2), axis=AX.X),
                  reads=[lw], writes=[lamt])
            kb.op("scalar", lambda e: e.activation(out=lamt.h[:, 4:6], in_=lamt.h[:, 2:4], func=AF.Exp), reads=[lamt], writes=[lamt])
            kb.op("vector", lambda e: e.scalar_tensor_tensor(out=lamt.h[:, 0:1], in0=lamt.h[:, 5:6], scalar=-lam_init,
                                                            in1=lamt.h[:, 4:5], op0=ALU.add, op1=ALU.subtract),
                  reads=[lamt], writes=[lamt])
            kb.op("vector", lambda e: e.tensor_scalar(out=gsl.h[:], in0=pf.h[:, pfo + PF_SLG:pfo + PF_SLG + 1],
                                                     scalar1=(1.0 - lam_init), scalar2=None, op0=ALU.mult),
                  reads=[pf], writes=[gsl])
            kb.barrier()

        if l + 1 < NL:
            kb._wait(kb.engs["gpsimd"], kb.engs["tensor"].chan, kb.engs["tensor"].chan.count)
            emit_casts(l + 1, ("mod", "inw", "ints", "out", "gate", "up", "down"), fence=False)
        a1 = lambda k, s: modA.h[:, 0, k, s:s + 1]
        b1 = lambda k, s: modT.h[:, 0 + k, s:s + 1]
        g1 = lambda k, s: modT.h[:, 32 + k, s:s + 1]
        a2 = lambda k, s: modA.h[:, 1, k, s:s + 1]
        b2 = lambda k, s: modT.h[:, 48 + k, s:s + 1]
        g2 = lambda k, s: modT.h[:, 80 + k, s:s + 1]

        if not dbg.get("skip_mix"):
            with ExitStack() as st:
                new_psum(st)
                xt = kb.sb(st, "a_xt", [128, KD, TT], F32, chan=kb.new_chan("a_xt"))
                actT = kb.sb(st, "a_act", [128, KD, TT], BF16)
                sqt = (kb.sb(st, "a_sq", [128, KD, TT], BF16), kb.sb(st, "a_sd", [128, TT], F32), kb.sb(st, "a_rs", [128, TT], F32))
                tmp = [kb.sb(st, f"a_tmp{i}", [128, TT], F32) for i in range(2)]
                wbufs = [kb.sb(st, f"a_w{i}", [128, KD * 512], BF16, chan=kb.new_chan(f"a_w{i}")) for i in range(2)]
                wts = kb.sb(st, "a_wts", [128, KD * 800], BF16, chan=kb.new_chan("a_wts"))
                cst = kb.sb(st, "a_cos", [128, TT], F32, chan=kb.new_chan("a_cs"))
                snt = kb.sb(st, "a_sin", [128, TT], F32, chan=cst.reg.chan)
                stg_z = kb.sb(st, "a_sz", [128, 8, TT], BF16)
                stg_x = kb.sb(st, "a_sx", [128, 12, TT], BF16)
                stg_q = kb.sb(st, "a_sq4", [128, 4, TT], BF16)
                stg_k = kb.sb(st, "a_sk", [128, 2, TT], BF16)
                stg_dq = kb.sb(st, "a_sdq", [128, 4, TT], BF16)
                stg_dk = kb.sb(st, "a_sdk", [128, 4, TT], BF16)
                stg_dt = kb.sb(st, "a_sdt", [128, 4, 32], F32)
                stg_vg = kb.sb(st, "a_svg", [128, 4, 256], BF16)
                stg_vd = kb.sb(st, "a_svd", [128, 4, 512], BF16)
                h_sq = kb.sb(st, "a_hsq", [128, 1, TT], BF16)
                h_sd = kb.sb(st, "a_hsd", [128, TT], F32)
                h_rs = kb.sb(st, "a_hrs", [128, TT], F32)
                h_qn = kb.sb(st, "a_hqn", [128, TT], BF16)
                h_t1 = kb.sb(st, "a_ht1", [128, TT], F32)
                h_t2 = kb.sb(st, "a_ht2", [128, TT], F32)
                kb.dma("sync", wts.h[:], W["ints"].h[0], reads=[W["ints"]], writes=[wts], chan=wts.reg.chan)
                wtv = wts.h[:].rearrange("p (k c) -> p k c", k=KD)
                xsv = xsrc.h.rearrange("(k p) t -> p k t", p=128)
                for t in range(NT):
                    s = t // (NT // 2)
                    ts_ = slice(t * TT, (t + 1) * TT)
                    kb.dma("sync", xt.h[:], xsv[:, :, ts_], reads=[xsrc], writes=[xt], chan=xt.reg.chan)
                    kb.dma("sync", cst.h[:], cos_in.h[:, ts_], reads=[], writes=[cst], chan=cst.reg.chan)
                    kb.dma("sync", snt.h[:], sin_in.h[:, ts_], reads=[], writes=[snt], chan=cst.reg.chan)
                    norm_mod(xt, actT, sqt, lambda k: a1(k, s), lambda k: b1(k, s), tmp)
                    for b in range(9):
                        buf = load_w(wbufs, W["inw"], b, KD * 512)
                        wv = buf.h[:].rearrange("p (k c) -> p k c", k=KD)
                        for cc in range(4):
                            m = b * 4 + cc
                            if m >= 34:
                                break
                            ps = kb.ps()
                            def mm(e, wv=wv, cc=cc, ps=ps):
                                ins = None
                                for k in range(KD):
                                    ins = e.matmul(ps.h[:], lhsT=wv[:, k, cc * 128:(cc + 1) * 128], rhs=actT.h[:, k, :],
                                                   start=(k == 0), stop=(k == KD - 1))
                                return ins
                            kb.op("tensor", mm, reads=[buf, actT], writes=[ps])
                            if m < 8:
                                dst, j = stg_z, m
                            elif m < 20:
                                dst, j = stg_x, m - 8
                            elif m < 24:
                                dst, j = stg_q, m - 20
                            elif m < 26:
                                dst, j = stg_k, m - 24
                            elif m < 30:
                                dst, j = stg_dq, m - 26
                            else:
                                dst, j = stg_dk, m - 30
                            if 20 <= m < 26:
                                gcol = pf.h[:, pfo + (PF_QG if m < 24 else PF_KG):pfo + (PF_QG if m < 24 else PF_KG) + 1]
                                kb.op("scalar", lambda e, ps=ps: e.activation(out=h_sq.h[:, 0, :], in_=ps.h[:], func=AF.Square),
                                      reads=[ps], writes=[h_sq])
                                ps2 = kb.ps()
                                kb.op("tensor", lambda e, ps2=ps2: e.matmul(ps2.h[:], lhsT=ones_bf.h[:], rhs=h_sq.h[:, 0, :], start=True, stop=True),
                                      reads=[h_sq, ones_bf], writes=[ps2])
                                kb.op("scalar", lambda e, ps2=ps2: e.activation(out=h_sd.h[:], in_=ps2.h[:], func=AF.Sqrt, bias=EPS, scale=1.0 / 128),
                                      reads=[ps2], writes=[h_sd])
                                kb.op("vector", lambda e: e.reciprocal(out=h_rs.h[:], in_=h_sd.h[:]), reads=[h_sd], writes=[h_rs])
                                kb.op("vector", lambda e, ps=ps, gcol=gcol: e.scalar_tensor_tensor(
                                    out=h_qn.h[:], in0=ps.h[:], scalar=gcol, in1=h_rs.h[:], op0=ALU.mult, op1=ALU.mult),
                                    reads=[ps, h_rs, pf], writes=[h_qn])
                                ps3 = kb.ps()
                                kb.op("tensor", lambda e, ps3=ps3: e.matmul(ps3.h[:], lhsT=rm_bf.h[:], rhs=h_qn.h[:], start=True, stop=True),
                                      reads=[h_qn, rm_bf], writes=[ps3])
                                kb.op("vector", lambda e: e.tensor_tensor(out=h_t1.h[:], in0=h_qn.h[:], in1=cst.h[:], op=ALU.mult),
                                      reads=[h_qn, cst], writes=[h_t1])
                                kb.op("vector", lambda e, ps3=ps3: e.tensor_tensor(out=h_t2.h[:], in0=ps3.h[:], in1=snt.h[:], op=ALU.mult),
                                      reads=[ps3, snt], writes=[h_t2])
                                kb.op("vector", lambda e, dst=dst, j=j: e.tensor_tensor(out=dst.h[:, j, :], in0=h_t1.h[:], in1=h_t2.h[:], op=ALU.add),
                                      reads=[h_t1, h_t2], writes=[dst])
                            else:
                                kb.op("scalar", lambda e, ps=ps, dst=dst, j=j: e.copy(out=dst.h[:, j, :], in_=ps.h[:]),
                                      reads=[ps], writes=[dst])
                    for tb in range(4):
                        psA = kb.ps()
                        psB = kb.ps()
                        def mm(e, tb=tb, psA=psA, psB=psB):
                            ins = None
                            for k in range(KD):
                                ins = e.matmul(psA.h[:, 0:288], lhsT=actT.h[:, k, tb * 128:(tb + 1) * 128], rhs=wtv[:, k, 0:288],
                                               start=(k == 0), stop=(k == KD - 1))
                            for k in range(KD):
                                ins = e.matmul(psB.h[:, 0:512], lhsT=actT.h[:, k, tb * 128:(tb + 1) * 128], rhs=wtv[:, k, 288:800],
                                               start=(k == 0), stop=(k == KD - 1))
                            return ins
                        kb.op("tensor", mm, reads=[actT, wts], writes=[psA, psB])
                        kb.op("vector", lambda e, tb=tb, psA=psA: e.tensor_copy(out=stg_dt.h[:, tb, :], in_=psA.h[:, 0:32]),
                              reads=[psA], writes=[stg_dt])
                        kb.op("vector", lambda e, tb=tb, psA=psA: e.tensor_copy(out=stg_vg.h[:, tb, :], in_=psA.h[:, 32:288]),
                              reads=[psA], writes=[stg_vg])
                        kb.op("scalar", lambda e, tb=tb, psB=psB: e.copy(out=stg_vd.h[:, tb, :], in_=psB.h[:, 0:512]),
                              reads=[psB], writes=[stg_vd])
                    for stg, dst, nchk in ((stg_z, zT, 8), (stg_x, xbcT, 12), (stg_q, qT, 4), (stg_k, kT, 2), (stg_dq, dqT, 4), (stg_dk, dkT, 4)):
                        kb.dma("sync", dst.h.rearrange("(c p) t -> p c t", p=128)[:, :, ts_], stg.h[:], reads=[stg], writes=[dst], chan=dst.reg.chan)
                    for stg, dst in ((stg_dt, dtm), (stg_vg, vg), (stg_vd, vd)):
                        kb.dma("sync", dst.h[ts_, :].rearrange("(b p) f -> p b f", p=128), stg.h[:], reads=[stg], writes=[dst], chan=dst.reg.chan)
                kb.barrier()

            if not dbg.get("skip_ssd"):
                emit_ssd(l, pfo, pbo)
            else:
                zero_mix(0, 8)
            if not dbg.get("skip_attn"):
                emit_attn(l, pfo)
            else:
                zero_mix(8, 16)

        with ExitStack() as st:
            new_psum(st)
            xt = kb.sb(st, "f_xt", [128, KD, TT], F32, chan=kb.new_chan("f_xt"))
            actT = kb.sb(st, "f_act", [128, KD, TT], BF16, chan=kb.new_chan("f_act"))
            sqt = (kb.sb(st, "f_sq", [128, KD, TT], BF16), kb.sb(st, "f_sd", [128, TT], F32), kb.sb(st, "f_rs", [128, TT], F32))
            tmp = [kb.sb(st, f"f_tmp{i}", [128, TT], F32) for i in range(2)]
            fT = kb.sb(st, "f_fT", [128, KF, TT], BF16)
            sgs = [kb.sb(st, f"f_sg{i}", [128, TT], F32) for i in range(2)]
            wbufs = [kb.sb(st, f"f_w{i}", [128, KD * 512], BF16, chan=kb.new_chan(f"f_w{i}")) for i in range(4)]
            xsv = xsrc.h.rearrange("(k p) t -> p k t", p=128)
            xdv = xTs.h.rearrange("(k p) t -> p k t", p=128)
            mxv = mixT.h.rearrange("(k p) t -> p k t", p=128)
            for t in range(NT):
                s = t // (NT // 2)
                ts_ = slice(t * TT, (t + 1) * TT)
                kb.dma("sync", xt.h[:], xsv[:, :, ts_], reads=[xsrc], writes=[xt], chan=xt.reg.chan)
                if not dbg.get("skip_mix"):
                    kb.dma("sync", actT.h[:], mxv[:, :, ts_], reads=[mixT], writes=[actT], chan=actT.reg.chan)
                    for b in range(4):
                        buf = load_w(wbufs, W["out"], b, KD * 512)
                        wv = buf.h[:].rearrange("p (k c) -> p k c", k=KD)
                        for cc in range(4):
                            m = b * 4 + cc
                            ps = kb.ps()
                            def mm(e, wv=wv, cc=cc, ps=ps):
                                ins = None
                                for k in range(KD):
                                    ins = e.matmul(ps.h[:], lhsT=wv[:, k, cc * 128:(cc + 1) * 128], rhs=actT.h[:, k, :],
                                                   start=(k == 0), stop=(k == KD - 1))
                                return ins
                            kb.op("tensor", mm, reads=[buf, actT], writes=[ps])
                            kb.op("vector", lambda e, m=m, ps=ps: e.scalar_tensor_tensor(
                                out=xt.h[:, m, :], in0=ps.h[:], scalar=g1(m, s), in1=xt.h[:, m, :], op0=ALU.mult, op1=ALU.add),
                                reads=[ps, xt, modT], writes=[xt])
                norm_mod(xt, actT, sqt, lambda k: a2(k, s), lambda k: b2(k, s), tmp)
                for b in range(11):
                    bg = load_w(wbufs, W["gate"], b, KD * 512)
                    bu = load_w(wbufs, W["up"], b, KD * 512)
                    wg = bg.h[:].rearrange("p (k c) -> p k c", k=KD)
                    wu = bu.h[:].rearrange("p (k c) -> p k c", k=KD)
                    for cc in range(4):
                        j = b * 4 + cc
                        psg = kb.ps()
                        psu = kb.ps()
                        def mm(e, wg=wg, wu=wu, cc=cc, psg=psg, psu=psu):
                            ins = None
                            for k in range(KD):
                                ins = e.matmul(psg.h[:], lhsT=wg[:, k, cc * 128:(cc + 1) * 128], rhs=actT.h[:, k, :],
                                               start=(k == 0), stop=(k == KD - 1))
                            for k in range(KD):
                                ins = e.matmul(psu.h[:], lhsT=wu[:, k, cc * 128:(cc + 1) * 128], rhs=actT.h[:, k, :],
                                               start=(k == 0), stop=(k == KD - 1))
                            return ins
                        kb.op("tensor", mm, reads=[bg, bu, actT], writes=[psg, psu])
                        sg = sgs[j % 2]
                        kb.op("scalar", lambda e, psg=psg, sg=sg: e.activation(out=sg.h[:], in_=psg.h[:], func=AF.Silu),
                              reads=[psg], writes=[sg])
                        kb.op("vector", lambda e, psu=psu, sg=sg, j=j: e.tensor_tensor(out=fT.h[:, j, :], in0=sg.h[:], in1=psu.h[:], op=ALU.mult),
                              reads=[psu, sg], writes=[fT])
                for m in range(16):
                    buf = load_w(wbufs, W["down"], m, KF * 128)
                    wv = buf.h[:, 0:KF * 128].rearrange("p (k c) -> p k c", k=KF)
                    ps = kb.ps()
                    def mm(e, wv=wv, ps=ps):
                        ins = None
                        for k in range(KF):
                            ins = e.matmul(ps.h[:], lhsT=wv[:, k, :], rhs=fT.h[:, k, :], start=(k == 0), stop=(k == KF - 1))
                        return ins
                    kb.op("tensor", mm, reads=[buf, fT], writes=[ps])
                    kb.op("vector", lambda e, m=m, ps=ps: e.scalar_tensor_tensor(
                        out=xt.h[:, m, :], in0=ps.h[:], scalar=g2(m, s), in1=xt.h[:, m, :], op0=ALU.mult, op1=ALU.add),
                        reads=[ps, xt, modT], writes=[xt])
                kb.dma("sync", xdv[:, :, ts_], xt.h[:], reads=[xt], writes=[xTs], chan=xTs.reg.chan)
            kb.barrier()

    with ExitStack() as st:
        new_psum(st)
        xt = kb.sb(st, "o_xt", [128, KD, TT], F32, chan=kb.new_chan("o_xt"))
        yt = kb.sb(st, "o_yt", [128, KD, TT], F32)
        sqt = (kb.sb(st, "o_sq", [128, KD, TT], BF16), kb.sb(st, "o_sd", [128, TT], F32), kb.sb(st, "o_rs", [128, TT], F32))
        xsv = xTs.h.rearrange("(k p) t -> p k t", p=128)
        yv = yT_out.h.rearrange("(k p) t -> p k t", p=128)
        fg0 = NL * NPF
        for t in range(NT):
            ts_ = slice(t * TT, (t + 1) * TT)
            kb.dma("sync", xt.h[:], xsv[:, :, ts_], reads=[xTs], writes=[xt], chan=xt.reg.chan)
            rstd = rms_bcast(sqt, lambda k: xt.h[:, k, :], KD, D, [xt])
            for k in range(KD):
                kb.op("vector", lambda e, k=k: e.scalar_tensor_tensor(out=yt.h[:, k, :], in0=xt.h[:, k, :], scalar=pf.h[:, fg0 + k:fg0 + k + 1],
                                                                     in1=rstd.h[:], op0=ALU.mult, op1=ALU.mult),
                      reads=[xt, rstd, pf], writes=[yt])
            kb.dma("sync", yv[:, :, ts_], yt.h[:], reads=[yt], writes=[yT_out], chan=yT_out.reg.chan)
        kb.barrier()
    es.close()
    return nc


def _tile_w(w, cw):
    K, N = w.shape
    nk = K // 128
    nb = N // cw
    return np.ascontiguousarray(w.reshape(nk, 128, nb, cw).transpose(2, 1, 0, 3)).reshape(nb, 128, nk * cw)


def _pcol(v):
    return np.ascontiguousarray(v.reshape(-1, 128).T)


def _host_consts(T, SEG):
    i = np.arange(128)
    UI = (i[:, None] <= i[None, :]).astype(np.float32)
    LS = (i[:, None] > i[None, :]).astype(np.float32)
    LI = (i[:, None] >= i[None, :]).astype(np.float32)
    US = (i[:, None] < i[None, :]).astype(np.float32)
    Rm = np.zeros((128, 128), np.float32)
    for base in (0, 64):
        for m in range(32):
            Rm[base + m + 32, base + m] = -1.0
            Rm[base + m, base + m + 32] = 1.0
    ident = np.eye(128, dtype=np.float32)
    consts = np.stack([UI, LS, LI, US, Rm, ident], axis=1)
    def tables(S):
        t = np.arange(S)
        row = (t // 64).astype(np.float32)
        col = (t % 64).astype(np.float32)
        inv = (10000.0 ** (-np.arange(0, 64, 2, dtype=np.float32) / 64)).astype(np.float32)
        ar = row[None, :] * inv[:, None]
        ac = col[None, :] * inv[:, None]
        ang = np.concatenate([ar, ar, ac, ac], axis=0)
        return np.cos(ang).astype(np.float32), np.sin(ang).astype(np.float32)
    NAL = 2 * T - 128
    OFFA = T - 128
    v = np.arange(NAL)[None, :] - OFFA - np.arange(128)[:, None]
    alibi = (-np.abs(v)).astype(np.float32)
    return consts, tables, alibi


def _prep_weights(inp, NL):
    f = np.float32
    out = {}
    w_in = inp["w_in"]
    cols_w = np.concatenate([np.arange(OFF_Z, OFF_DT), np.arange(OFF_GQ, OFF_GV), np.arange(OFF_DQ, OFF_DV)])
    cols_t = np.concatenate([np.arange(OFF_DT, OFF_GQ), np.arange(OFF_GV, OFF_DQ), np.arange(OFF_DV, OFF_DV + 512)])
    inw = np.zeros((NL, D, 9 * 512), f)
    inw[:, :, :cols_w.size] = w_in[:NL][:, :, cols_w]
    out["w_inw_t"] = np.stack([_tile_w(inw[l], 512) for l in range(NL)])
    out["w_ints_t"] = np.stack([_tile_w(np.ascontiguousarray(w_in[l][:, cols_t]), 800)[0] for l in range(NL)])
    out["w_mod_t"] = np.stack([_tile_w(inp["w_mod"][l], 512) for l in range(NL)])
    out["w_out_t"] = np.stack([_tile_w(inp["w_out"][l], 512) for l in range(NL)])
    out["w_gate_t"] = np.stack([_tile_w(inp["w_gate"][l], 512) for l in range(NL)])
    out["w_up_t"] = np.stack([_tile_w(inp["w_up"][l], 512) for l in range(NL)])
    out["w_down_t"] = np.stack([_tile_w(inp["w_down"][l], 128) for l in range(NL)])
    pf = np.zeros((128, NL * NPF + 16), f)
    pb = np.zeros((128, NL * NPB), f)
    for l in range(NL):
        o = l * NPF
        pf[:, o + PF_N1:o + PF_N1 + 16] = _pcol(inp["norm1_g"][l])
        pf[:, o + PF_N2:o + PF_N2 + 16] = _pcol(inp["norm2_g"][l])
        pf[:, o + PF_BMOD:o + PF_BMOD + 96] = _pcol(inp["b_mod"][l])
        for k in range(5):
            pf[:, o + PF_CW + k * 12:o + PF_CW + (k + 1) * 12] = _pcol(inp["conv_w"][l][k])
        pf[:, o + PF_CB:o + PF_CB + 12] = _pcol(inp["conv_b"][l])
        pf[:, o + PF_SNG:o + PF_SNG + 8] = _pcol(inp["ssd_norm_g"][l])
        pf[:, o + PF_DSK:o + PF_DSK + 8] = _pcol(np.repeat(inp["d_skip"][l], 64))
        pf[:, o + PF_QG] = inp["q_norm_g"][l]
        pf[:, o + PF_KG] = inp["k_norm_g"][l]
        pf[:, o + PF_SLG] = inp["diff_subln_g"][l]
        ob = l * NPB
        pb[:, ob + PB_DTB:ob + PB_DTB + 32] = inp["dt_bias"][l].reshape(1, 32)
        pb[:, ob + PB_ALOG:ob + PB_ALOG + 32] = inp["a_log"][l].reshape(1, 32)
        pb[:, ob + PB_LAM:ob + PB_LAM + 256] = inp["diff_lambda"][l].reshape(1, 256)
    pf[:, NL * NPF:] = _pcol(inp["final_g"])
    out["pf"] = pf
    out["pb"] = pb
    return out


_CACHE = {}


def run(inp, SEG, NL, dbg=None, trace=False):
    T = 2 * SEG
    xp, xs = np.asarray(inp["x_prompt"]), np.asarray(inp["x_sample"])
    cp, cs = np.asarray(inp["c_prompt"]), np.asarray(inp["c_sample"])
    Bp, Bs = xp.shape[0], xs.shape[0]
    inp = {k: np.asarray(v) for k, v in inp.items()}
    items = [("p", i) for i in range(Bp)] + [("s", i) for i in range(0, Bs, 2)]
    assert len(items) <= NCORES
    shared = _prep_weights(inp, NL)
    consts, tables, alibi = _host_consts(T, SEG)
    cos_p, sin_p = tables(T)
    cos_s, sin_s = tables(SEG)
    shared["consts"] = consts
    shared["alibi"] = alibi
    in_maps = []
    for c in range(NCORES):
        kind, i = items[c] if c < len(items) else items[0]
        m = dict(shared)
        if kind == "p":
            m["xT"] = np.ascontiguousarray(xp[i].T)
            cc = np.stack([cp[i], cp[i]], axis=0)
            m["flags"] = np.tile(np.array([[1.0, 0.0]], np.float32), (128, 1))
            m["cosT"], m["sinT"] = cos_p, sin_p
        else:
            m["xT"] = np.ascontiguousarray(np.concatenate([xs[i], xs[i + 1]], axis=0).T)
            cc = np.stack([cs[i], cs[i + 1]], axis=0)
            m["flags"] = np.tile(np.array([[0.0, -30000.0]], np.float32), (128, 1))
            m["cosT"] = np.concatenate([cos_s, cos_s], axis=1)
            m["sinT"] = np.concatenate([sin_s, sin_s], axis=1)
        m["cT"] = np.ascontiguousarray(cc.reshape(2, KD, 128).transpose(2, 1, 0)).reshape(128, 32)
        in_maps.append(m)
    key = (T, NL, tuple(sorted((dbg or {}).items())))
    if key not in _CACHE:
        _CACHE[key] = build_program(T, NL, dbg)
    nc = _CACHE[key]
    res = run_bass_kernel_spmd(nc, in_maps, core_ids=list(range(NCORES)), trace=trace)
    yp = np.zeros_like(xp)
    ys = np.zeros_like(xs)
    for c, (kind, i) in enumerate(items):
        y = res.results[c]["yT"].T
        if kind == "p":
            yp[i] = y
        else:
            ys[i] = y[:SEG]
            ys[i + 1] = y[SEG:]
    return (yp, ys), res


def kernel(**inputs):
    (yp, ys), _ = run(inputs, 2048, 4)
    return yp, ys
```

```python
import math
from contextlib import ExitStack

import numpy as np

import concourse.bass as bass
import concourse.mybir as mybir
from concourse.bass_utils import run_bass_kernel_spmd

F32 = mybir.dt.float32
BF16 = mybir.dt.bfloat16
AF = mybir.ActivationFunctionType
ALU = mybir.AluOpType
AX = mybir.AxisListType

D = 2048
KD = 16
DFF = 5632
KF = 44
EPS = 1e-6
NCORES = 8
TT = 512
OFF_Z, OFF_XBC, OFF_DT, OFF_GQ, OFF_GK, OFF_GV, OFF_DQ, OFF_DK, OFF_DV = 0, 1024, 2560, 2592, 3104, 3360, 3616, 4128, 4640
NPF = 219
PF_N1, PF_N2, PF_BMOD, PF_CW, PF_CB, PF_SNG, PF_DSK, PF_QG, PF_KG, PF_SLG = 0, 16, 32, 128, 188, 200, 208, 216, 217, 218
NPB = 320
PB_DTB, PB_ALOG, PB_LAM = 0, 32, 64


class Chan:
    def __init__(self, sem, is_dma):
        self.sem = sem
        self.count = 0
        self.is_dma = is_dma


class Region:
    def __init__(self, name, chan=None):
        self.name = name
        self.w = None
        self.r = {}
        self.chan = chan


class Tile:
    def __init__(self, h, reg):
        self.h = h
        self.reg = reg


class Eng:
    def __init__(self, name, h, chan):
        self.name = name
        self.h = h
        self.chan = chan
        self.waited = {}


def _reg(x):
    return x.reg if isinstance(x, Tile) else x


class KB:
    def __init__(self, nc):
        self.nc = nc
        self.es = ExitStack()
        self.chans = []
        self.engs = {}
        for n in ("tensor", "vector", "scalar", "gpsimd", "sync"):
            self.engs[n] = Eng(n, getattr(nc, n), self.new_chan("e_" + n, False))
        self.ps_rr = 0
        self.psb = []

    def new_chan(self, name, is_dma=True):
        if not hasattr(self, "cmap"):
            self.cmap = {}
        if name in self.cmap:
            return self.cmap[name]
        sem = self.es.enter_context(self.nc.semaphore("sem_" + name))
        c = Chan(sem, is_dma)
        self.chans.append(c)
        self.cmap[name] = c
        return c

    def region(self, name, chan=None):
        return Region(name, chan)

    def dram(self, name, shape, dt, kind="Internal", chan=None):
        t = self.nc.dram_tensor(name, list(shape), dt, kind=kind).ap()
        return Tile(t, Region(name, chan))

    def sb(self, stack, name, shape, dt, chan=None):
        self.uid = getattr(self, "uid", 0) + 1
        name = f"sb_{name}_{self.uid}"
        h = stack.enter_context(self.nc.sbuf_tensor(name, list(shape), dt))
        return Tile(h, Region(name, chan))

    def psum(self, stack, name, shape, dt=F32):
        self.uid = getattr(self, "uid", 0) + 1
        name = f"pp_{name}_{self.uid}"
        h = stack.enter_context(self.nc.psum_tensor(name, list(shape), dt))
        return Tile(h, Region(name))

    def _wait(self, eng, chan, val):
        if chan.is_dma:
            val = chan.count
        if eng.waited.get(chan, 0) >= val:
            return
        eng.h.wait_ge(chan.sem, val)
        eng.waited[chan] = val

    def op(self, engname, fn, reads=(), writes=(), chan=None):
        eng = self.engs[engname]
        evs = []
        for r in reads:
            r = _reg(r)
            if r.w is not None:
                evs.append(r.w)
        for w in writes:
            w = _reg(w)
            if w.w is not None:
                evs.append(w.w)
            evs.extend(w.r.items())
        for (c, v) in evs:
            if c is eng.chan and engname == "tensor":
                continue
            self._wait(eng, c, v)
        ins = fn(eng.h)
        c = chan if chan is not None else eng.chan
        inc = 16 if c.is_dma else 1
        c.count += inc
        ins.then_inc(c.sem, inc)
        for w in writes:
            w = _reg(w)
            w.w = (c, c.count)
            w.r = {}
        for r in reads:
            r = _reg(r)
            if r.r.get(c, 0) < c.count:
                r.r[c] = c.count
        return ins

    def dma(self, q, out, in_, reads, writes, chan):
        return self.op(q, lambda e: e.dma_start(out=out, in_=in_), reads=reads, writes=writes, chan=chan)

    def barrier(self):
        for eng in self.engs.values():
            for c in self.chans:
                if c.count > 0 and eng.waited.get(c, 0) < c.count:
                    eng.h.wait_ge(c.sem, c.count)
                    eng.waited[c] = c.count

    def ps(self):
        t = self.psb[self.ps_rr % len(self.psb)]
        self.ps_rr += 1
        return t


def build_program(T, NL, dbg=None):
    dbg = dbg or {}
    SEG = T // 2
    NT = T // TT
    NCH = T // 128
    NAL = 2 * T - 128
    nc = bass.Bass("TRN2", target_bir_lowering=False)
    kb = KB(nc)
    es = kb.es

    def inp(name, shape, dt=F32):
        return kb.dram(name, shape, dt, kind="ExternalInput")

    xT_in = inp("xT", [D, T])
    cT_in = inp("cT", [128, 32])
    flags_in = inp("flags", [128, 2])
    cos_in = inp("cosT", [128, T])
    sin_in = inp("sinT", [128, T])
    alibi_in = inp("alibi", [128, NAL])
    consts_in = inp("consts", [128, 6, 128])
    pf_in = inp("pf", [128, NL * NPF + 16])
    pb_in = inp("pb", [128, NL * NPB])
    w_mod_in = inp("w_mod_t", [NL, 24, 128, KD * 512])
    w_inw_in = inp("w_inw_t", [NL, 9, 128, KD * 512])
    w_ints_in = inp("w_ints_t", [NL, 128, KD * 800])
    w_out_in = inp("w_out_t", [NL, 4, 128, KD * 512])
    w_gate_in = inp("w_gate_t", [NL, 11, 128, KD * 512])
    w_up_in = inp("w_up_t", [NL, 11, 128, KD * 512])
    w_down_in = inp("w_down_t", [NL, 16, 128, KF * 128])
    yT_out = kb.dram("yT", [D, T], F32, kind="ExternalOutput", chan=kb.new_chan("yT"))

    wsc = []
    for l in range(NL):
        d = {}
        for nm, src, shp in (("mod", w_mod_in, [24, 128, KD * 512]), ("inw", w_inw_in, [9, 128, KD * 512]),
                             ("ints", w_ints_in, [1, 128, KD * 800]), ("out", w_out_in, [4, 128, KD * 512]),
                             ("gate", w_gate_in, [11, 128, KD * 512]), ("up", w_up_in, [11, 128, KD * 512]),
                             ("down", w_down_in, [16, 128, KF * 128])):
            d[nm] = kb.dram(f"wb_{nm}{l}", shp, BF16, chan=kb.new_chan(f"wc_{nm}{l}"))
            d[nm].src = src
        wsc.append(d)

    def scr(name, shape, dt):
        return kb.dram(name, shape, dt, chan=kb.new_chan("s_" + name))

    xTs = scr("xTs", [D, T], F32)
    zT = scr("zT", [1024, T], BF16)
    xbcT = scr("xbcT", [1536, T], BF16)
    qT = scr("qT", [512, T], BF16)
    kT = scr("kT", [256, T], BF16)
    dqT = scr("dqT", [512, T], BF16)
    dkT = scr("dkT", [512, T], BF16)
    dtm = scr("dtm", [T, 32], F32)
    vg = scr("vg", [T, 256], BF16)
    vd = scr("vd", [T, 512], BF16)
    xcT = scr("xcT", [1536, T], BF16)
    Xtok = scr("Xtok", [T, 1024], BF16)
    Btok = scr("Btok", [T, 256], BF16)
    mixT = scr("mixT", [D, T], BF16)

    def emit_casts(l, names, fence=True):
        gp = kb.engs["gpsimd"]
        for nm in names:
            w = wsc[l][nm]
            nb = w.h.shape[0]
            for b in range(nb):
                src = w.src.h[l, b] if nm != "ints" else w.src.h[l]
                kb.dma("gpsimd", w.h[b], src, reads=[], writes=[w], chan=w.reg.chan)
            if fence:
                kb._wait(gp, w.reg.chan, w.reg.chan.count)

    emit_casts(0, ("mod", "inw", "ints", "out", "gate", "up", "down"))

    pst = ExitStack()
    es.enter_context(pst)
    ld = kb.new_chan("ld_const")
    consts_f = kb.sb(pst, "consts_f", [128, 6, 128], F32, chan=ld)
    pf = kb.sb(pst, "pf", [128, NL * NPF + 16], F32, chan=ld)
    pb = kb.sb(pst, "pb", [128, NL * NPB], F32, chan=ld)
    flags = kb.sb(pst, "flags", [128, 2], F32, chan=ld)
    cTs = kb.sb(pst, "cTs", [128, 32], F32, chan=ld)
    for tl, src in ((consts_f, consts_in), (pf, pf_in), (pb, pb_in), (flags, flags_in), (cTs, cT_in)):
        kb.dma("sync", tl.h[:], src.h[:], reads=[], writes=[tl], chan=ld)
    ones_bf = kb.sb(pst, "ones_bf", [128, 128], BF16)
    ones_f = kb.sb(pst, "ones_f", [128, 128], F32)
    ident_bf = kb.sb(pst, "ident_bf", [128, 128], BF16)
    rm_bf = kb.sb(pst, "rm_bf", [128, 128], BF16)
    silc = kb.sb(pst, "silc", [128, 32], BF16)
    modT = kb.sb(pst, "modT", [128, 96, 2], F32)
    modA = kb.sb(pst, "modA", [128, 2, KD, 2], F32)
    lamt = kb.sb(pst, "lamt", [128, 8], F32)
    gsl = kb.sb(pst, "gsl", [128, 1], F32)
    kb.op("vector", lambda e: e.memset(ones_bf.h[:], 1.0), writes=[ones_bf])
    kb.op("vector", lambda e: e.memset(ones_f.h[:], 1.0), writes=[ones_f])
    kb.op("vector", lambda e: e.tensor_copy(out=ident_bf.h[:], in_=consts_f.h[:, 5, :]), reads=[consts_f], writes=[ident_bf])
    kb.op("vector", lambda e: e.tensor_copy(out=rm_bf.h[:], in_=consts_f.h[:, 4, :]), reads=[consts_f], writes=[rm_bf])
    kb.op("scalar", lambda e: e.activation(out=silc.h[:], in_=cTs.h[:], func=AF.Silu), reads=[cTs], writes=[silc])
    TRI = [consts_f.h[:, 0, :], consts_f.h[:, 2, :]]
    STRICT = [consts_f.h[:, 1, :], consts_f.h[:, 3, :]]
    flag_ap = flags.h[:, 0:1]
    mask_ap = flags.h[:, 1:2]

    def new_psum(stack):
        kb.psb = [kb.psum(stack, f"ps{i}", [128, 512]) for i in range(8)]
        kb.ps_rr = 0

    def rms_bcast(stack_tiles, src_fn, nchunk, dim, srcs, src_all=None):
        sq, sd, rstd = stack_tiles
        pss = kb.ps()
        if src_all is not None:
            kb.op("scalar", lambda e: e.activation(out=sq.h[:], in_=src_all, func=AF.Square), reads=srcs, writes=[sq])
        else:
            for k in range(nchunk):
                kb.op("scalar", lambda e, k=k: e.activation(out=sq.h[:, k, :], in_=src_fn(k), func=AF.Square),
                      reads=srcs, writes=[sq])
        def mm(e):
            ins = None
            for k in range(nchunk):
                ins = e.matmul(pss.h[:], lhsT=ones_bf.h[:], rhs=sq.h[:, k, :], start=(k == 0), stop=(k == nchunk - 1))
            return ins
        kb.op("tensor", mm, reads=[sq, ones_bf], writes=[pss])
        kb.op("scalar", lambda e: e.activation(out=sd.h[:], in_=pss.h[:], func=AF.Sqrt, bias=EPS, scale=1.0 / dim),
              reads=[pss], writes=[sd])
        kb.op("vector", lambda e: e.reciprocal(out=rstd.h[:], in_=sd.h[:]), reads=[sd], writes=[rstd])
        return rstd

    wrr = [0]

    def load_w(wbufs, wt, b, ncols):
        buf = wbufs[wrr[0] % len(wbufs)]
        wrr[0] += 1
        kb.dma("sync", buf.h[:, 0:ncols], wt.h[b], reads=[wt], writes=[buf], chan=buf.reg.chan)
        return buf

    def norm_mod(xt, actT, sqt, a_fn, b_fn, tmp):
        rstd = rms_bcast(sqt, lambda k: xt.h[:, k, :], KD, D, [xt], src_all=xt.h[:])
        for k in range(KD):
            tb = tmp[k % 2]
            kb.op("vector", lambda e, k=k, tb=tb: e.scalar_tensor_tensor(out=tb.h[:], in0=xt.h[:, k, :], scalar=a_fn(k),
                                                                        in1=rstd.h[:], op0=ALU.mult, op1=ALU.mult),
                  reads=[xt, rstd, modA, modT], writes=[tb])
            kb.op("scalar", lambda e, k=k, tb=tb: e.activation(out=actT.h[:, k, :], in_=tb.h[:], func=AF.Identity,
                                                              bias=b_fn(k), scale=1.0),
                  reads=[tb, modT], writes=[actT])


    def zero_mix(c0, c1):
        with ExitStack() as st:
            zt = kb.sb(st, "zz", [128, c1 - c0, TT], BF16)
            kb.op("vector", lambda e: e.memset(zt.h[:], 0.0), writes=[zt])
            mv = mixT.h.rearrange("(k p) t -> p k t", p=128)
            for t in range(NT):
                kb.dma("sync", mv[:, c0:c1, t * TT:(t + 1) * TT], zt.h[:], reads=[zt], writes=[mixT], chan=mixT.reg.chan)
            kb.barrier()

    OFFA = T - 128

    def emit_attn(l, pfo):
        with ExitStack() as st:
            acc = [kb.psum(st, f"acc{i}", [128, 512]) for i in range(4)]
            scb = [kb.psum(st, f"scb{i}", [128, 512]) for i in range(4)]
            kt = kb.sb(st, "c_kt", [128, T], BF16, chan=kb.new_chan("c_kt"))
            vt = kb.sb(st, "c_vt", [128, NCH, 128], BF16, chan=kb.new_chan("c_vt"))
            qts = [kb.sb(st, f"c_qt{i}", [128, TT], BF16, chan=kb.new_chan(f"c_qt{i}")) for i in range(2)]
            pTs = [kb.sb(st, f"c_p{i}", [128, TT], BF16) for i in range(6)]
            sbs = [kb.sb(st, f"c_sb{i}", [128, TT], F32) for i in range(6)]
            alb = kb.sb(st, "c_alb", [128, NAL], F32, chan=kb.new_chan("c_alb"))
            rr = [kb.sb(st, f"c_r{i}", [128, TT], F32) for i in range(2)]
            tt_ = [kb.sb(st, f"c_t{i}", [128, TT], F32) for i in range(2)]
            osb = kb.sb(st, "c_o", [128, TT], F32)
            osq = kb.sb(st, "c_osq", [128, TT], BF16)
            osd = kb.sb(st, "c_osd", [128, TT], F32)
            ors = kb.sb(st, "c_ors", [128, TT], F32)
            ostg = [kb.sb(st, f"c_os{i}", [128, TT], BF16) for i in range(2)]
            kb.dma("sync", alb.h[:], alibi_in.h[:], reads=[], writes=[alb], chan=alb.reg.chan)
            cnt = [0, 0, 0, 0]

            def core(qt, r0, r1, slope, scale, t, o_ps, s_ps):
                banks = {}

                def qk2(cp):
                    b0 = scb[2 * (cnt[0] % 2)]
                    b1 = scb[2 * (cnt[0] % 2) + 1]
                    cnt[0] += 1
                    banks[2 * cp] = b0
                    banks[2 * cp + 1] = b1
                    def mm(e, cp=cp, b0=b0, b1=b1):
                        e.matmul(b0.h[:], lhsT=kt.h[r0:r1, (2 * cp) * 128:(2 * cp + 1) * 128], rhs=qt.h[r0:r1, :], start=True, stop=True)
                        return e.matmul(b1.h[:], lhsT=kt.h[r0:r1, (2 * cp + 1) * 128:(2 * cp + 2) * 128], rhs=qt.h[r0:r1, :], start=True, stop=True)
                    kb.op("tensor", mm, reads=[kt, qt], writes=[b0, b1])
                NP = NCH // 2
                qk2(0)
                if NP > 1:
                    qk2(1)
                for cp in range(NP):
                    ps_ = []
                    for c in (2 * cp, 2 * cp + 1):
                        sc_ps = banks.pop(c)
                        cross = (c // (NCH // 2)) != (t // (NT // 2))
                        bias = mask_ap if cross else 0.0
                        if slope is not None:
                            sbt = sbs[cnt[1] % 6]
                            cnt[1] += 1
                            off = t * TT - c * 128 + OFFA
                            kb.op("vector", lambda e, sbt=sbt, off=off, sc_ps=sc_ps: e.scalar_tensor_tensor(
                                out=sbt.h[:], in0=alb.h[:, off:off + TT], scalar=slope / scale, in1=sc_ps.h[:], op0=ALU.mult, op1=ALU.add),
                                reads=[alb, sc_ps], writes=[sbt])
                            src = sbt
                        else:
                            src = sc_ps
                        p = pTs[cnt[2] % 6]
                        cnt[2] += 1
                        ps_.append(p)
                        kb.op("scalar", lambda e, p=p, src=src, bias=bias: e.activation(out=p.h[:], in_=src.h[:], func=AF.Exp, bias=bias, scale=scale),
                              reads=[src, flags], writes=[p])
                    if cp + 2 < NP:
                        qk2(cp + 2)
                    def mm2(e, cp=cp, ps_=ps_):
                        ins = None
                        for q, c in enumerate((2 * cp, 2 * cp + 1)):
                            e.matmul(o_ps.h[:], lhsT=vt.h[:, c, :], rhs=ps_[q].h[:], start=(c == 0), stop=(c == NCH - 1))
                            ins = e.matmul(s_ps.h[:], lhsT=ones_bf.h[:], rhs=ps_[q].h[:], start=(c == 0), stop=(c == NCH - 1))
                        return ins
                    kb.op("tensor", mm2, reads=[vt, ps_[0], ps_[1], ones_bf], writes=[o_ps, s_ps])

            mv = mixT.h.rearrange("(k p) t -> p k t", p=128)
            qi = 0
            for g in range(2):
                kb.dma("sync", kt.h[:], kT.h[g * 128:(g + 1) * 128, :], reads=[kT], writes=[kt], chan=kt.reg.chan)
                kb.dma("sync", vt.h[:], vg.h[:, g * 128:(g + 1) * 128].rearrange("(c p) d -> p c d", p=128), reads=[vg], writes=[vt], chan=vt.reg.chan)
                for hq in range(2):
                    h = 2 * g + hq
                    for t in range(NT):
                        ts_ = slice(t * TT, (t + 1) * TT)
                        qt = qts[qi % 2]
                        qi += 1
                        kb.dma("sync", qt.h[:], qT.h[h * 128:(h + 1) * 128, ts_], reads=[qT], writes=[qt], chan=qt.reg.chan)
                        o_ps, s_ps = acc[2 * (qi % 2)], acc[2 * (qi % 2) + 1]
                        core(qt, 0, 128, None, 128 ** -0.5, t, o_ps, s_ps)
                        r = rr[qi % 2]
                        og = ostg[qi % 2]
                        kb.op("vector", lambda e, r=r, s_ps=s_ps: e.reciprocal(out=r.h[:], in_=s_ps.h[:]), reads=[s_ps], writes=[r])
                        kb.op("vector", lambda e, r=r, o_ps=o_ps, og=og: e.tensor_tensor(out=og.h[:], in0=o_ps.h[:], in1=r.h[:], op=ALU.mult),
                              reads=[o_ps, r], writes=[og])
                        kb.dma("sync", mv[:, 8 + h, ts_], og.h[:], reads=[og], writes=[mixT], chan=mixT.reg.chan)
            for h in range(4):
                slope = 2.0 ** (-2.0 * (h + 1))
                kb.dma("sync", kt.h[:], dkT.h[h * 128:(h + 1) * 128, :], reads=[dkT], writes=[kt], chan=kt.reg.chan)
                kb.dma("sync", vt.h[:], vd.h[:, h * 128:(h + 1) * 128].rearrange("(c p) d -> p c d", p=128), reads=[vd], writes=[vt], chan=vt.reg.chan)
                for t in range(NT):
                    ts_ = slice(t * TT, (t + 1) * TT)
                    qt = qts[qi % 2]
                    qi += 1
                    kb.dma("sync", qt.h[:], dqT.h[h * 128:(h + 1) * 128, ts_], reads=[dqT], writes=[qt], chan=qt.reg.chan)
                    core(qt, 0, 64, slope, 0.125, t, acc[0], acc[1])
                    core(qt, 64, 128, slope, 0.125, t, acc[2], acc[3])
                    for i in range(2):
                        kb.op("vector", lambda e, i=i: e.reciprocal(out=rr[i].h[:], in_=acc[2 * i + 1].h[:]), reads=[acc[2 * i + 1]], writes=[rr[i]])
                        kb.op("vector", lambda e, i=i: e.tensor_tensor(out=tt_[i].h[:], in0=acc[2 * i].h[:], in1=rr[i].h[:], op=ALU.mult),
                              reads=[acc[2 * i], rr[i]], writes=[tt_[i]])
                    kb.op("vector", lambda e: e.scalar_tensor_tensor(out=osb.h[:], in0=tt_[1].h[:], scalar=lamt.h[:, 0:1], in1=tt_[0].h[:],
                                                                    op0=ALU.mult, op1=ALU.add), reads=[tt_[0], tt_[1], lamt], writes=[osb])
                    kb.op("scalar", lambda e: e.activation(out=osq.h[:], in_=osb.h[:], func=AF.Square), reads=[osb], writes=[osq])
                    nps = scb[2 * (cnt[0] % 2)]
                    cnt[0] += 1
                    kb.op("tensor", lambda e, nps=nps: e.matmul(nps.h[:], lhsT=ones_bf.h[:], rhs=osq.h[:], start=True, stop=True), reads=[osq, ones_bf], writes=[nps])
                    kb.op("scalar", lambda e, nps=nps: e.activation(out=osd.h[:], in_=nps.h[:], func=AF.Sqrt, bias=EPS, scale=1.0 / 128), reads=[nps], writes=[osd])
                    kb.op("vector", lambda e: e.reciprocal(out=ors.h[:], in_=osd.h[:]), reads=[osd], writes=[ors])
                    og = ostg[qi % 2]
                    kb.op("vector", lambda e, og=og: e.scalar_tensor_tensor(out=og.h[:], in0=osb.h[:], scalar=gsl.h[:, 0:1], in1=ors.h[:],
                                                                          op0=ALU.mult, op1=ALU.mult), reads=[osb, gsl, ors], writes=[og])
                    kb.dma("sync", mv[:, 12 + h, ts_], og.h[:], reads=[og], writes=[mixT], chan=mixT.reg.chan)
            kb.barrier()

    def emit_ssd(l, pfo, pbo):
        with ExitStack() as st:
            ptx = kb.psum(st, "ptx", [128, 1024], BF16)
            ptb = kb.psum(st, "ptb", [128, 256], BF16)
            xp = kb.sb(st, "b_xp", [128, 12, TT + 4], BF16, chan=kb.new_chan("b_xp"))
            acc = kb.sb(st, "b_acc", [128, 12, TT], F32)
            xc = kb.sb(st, "b_xc", [128, 12, TT], BF16)
            sX = kb.sb(st, "b_sX", [128, 4, 1024], BF16)
            sB = kb.sb(st, "b_sB", [128, 4, 256], BF16)
            xv = xbcT.h.rearrange("(c p) t -> p c t", p=128)
            cw = lambda k, c: pf.h[:, pfo + PF_CW + k * 12 + c:pfo + PF_CW + k * 12 + c + 1]
            cb = lambda c: pf.h[:, pfo + PF_CB + c:pfo + PF_CB + c + 1]
            for t in range(NT):
                lo = max(t * TT - 2, 0)
                hi = min(t * TT + TT + 2, T)
                d0 = lo - (t * TT - 2)
                kb.dma("sync", xp.h[:, :, d0:d0 + (hi - lo)], xv[:, :, lo:hi], reads=[xbcT], writes=[xp], chan=xp.reg.chan)
                if t == 0:
                    kb.op("vector", lambda e: e.memset(xp.h[:, :, 0:2], 0.0), writes=[xp])
                if t == NT - 1:
                    kb.op("vector", lambda e: e.memset(xp.h[:, :, TT + 2:TT + 4], 0.0), writes=[xp])
                if t == NT // 2:
                    kb.op("vector", lambda e: e.tensor_scalar(out=xp.h[:, :, 0:2], in0=xp.h[:, :, 0:2], scalar1=flag_ap, scalar2=None, op0=ALU.mult),
                          reads=[xp, flags], writes=[xp])
                if t == NT // 2 - 1:
                    kb.op("vector", lambda e: e.tensor_scalar(out=xp.h[:, :, TT + 2:TT + 4], in0=xp.h[:, :, TT + 2:TT + 4], scalar1=flag_ap, scalar2=None, op0=ALU.mult),
                          reads=[xp, flags], writes=[xp])
                for c in range(12):
                    kb.op("vector", lambda e, c=c: e.tensor_scalar(out=acc.h[:, c, :], in0=xp.h[:, c, 0:TT], scalar1=cw(0, c), scalar2=cb(c),
                                                                  op0=ALU.mult, op1=ALU.add), reads=[xp, pf], writes=[acc])
                    for k in range(1, 5):
                        kb.op("vector", lambda e, c=c, k=k: e.scalar_tensor_tensor(out=acc.h[:, c, :], in0=xp.h[:, c, k:k + TT], scalar=cw(k, c),
                                                                                  in1=acc.h[:, c, :], op0=ALU.mult, op1=ALU.add),
                              reads=[xp, pf, acc], writes=[acc])
                kb.op("scalar", lambda e: e.activation(out=xc.h[:], in_=acc.h[:], func=AF.Silu), reads=[acc], writes=[xc])
                kb.dma("sync", xcT.h.rearrange("(c p) t -> p c t", p=128)[:, :, t * TT:(t + 1) * TT], xc.h[:], reads=[xc], writes=[xcT], chan=xcT.reg.chan)
                for tb in range(4):
                    def tr(e, tb=tb):
                        ins = None
                        for c in range(8):
                            ins = e.transpose(ptx.h[:, c * 128:(c + 1) * 128], xc.h[:, c, tb * 128:(tb + 1) * 128], ident_bf.h[:])
                        for c in range(2):
                            ins = e.transpose(ptb.h[:, c * 128:(c + 1) * 128], xc.h[:, 8 + c, tb * 128:(tb + 1) * 128], ident_bf.h[:])
                        return ins
                    kb.op("tensor", tr, reads=[xc, ident_bf], writes=[ptx, ptb])
                    kb.op("vector", lambda e, tb=tb: e.tensor_copy(out=sX.h[:, tb, :], in_=ptx.h[:]), reads=[ptx], writes=[sX])
                    kb.op("scalar", lambda e, tb=tb: e.copy(out=sB.h[:, tb, :], in_=ptb.h[:]), reads=[ptb], writes=[sB])
                kb.dma("sync", Xtok.h[t * TT:(t + 1) * TT, :].rearrange("(b p) f -> p b f", p=128), sX.h[:], reads=[sX], writes=[Xtok], chan=Xtok.reg.chan)
                kb.dma("sync", Btok.h[t * TT:(t + 1) * TT, :].rearrange("(b p) f -> p b f", p=128), sB.h[:], reads=[sB], writes=[Btok], chan=Btok.reg.chan)
            kb.barrier()
        sto = ExitStack()
        ytot = kb.sb(sto, "s_ytot", [128, NCH, 1024], BF16)
        with ExitStack() as st:
            ps2 = [kb.psum(st, f"ps2{i}", [128, 1024]) for i in range(2)]
            psy = kb.psum(st, "psy", [128, 512])
            pso = kb.psum(st, "pso", [128, 512])
            pss_ = kb.psum(st, "pst", [128, 512])
            psm = kb.psum(st, "psm", [128, 512])
            state = [[kb.sb(st, f"s_st{d}{g}", [128, 8, 64], F32) for g in range(2)] for d in range(2)]
            prevb = [[kb.sb(st, f"s_pv{d}{g}", [128, 512], BF16) for g in range(2)] for d in range(2)]
            ab = kb.sb(st, "s_ab", [128, 32], F32)
            ldc = [kb.new_chan(f"s_ld{i}") for i in range(2)]
            Xc = [kb.sb(st, f"s_X{i}", [128, 16, 64], BF16, chan=ldc[i]) for i in range(2)]
            Bc = [kb.sb(st, f"s_B{i}", [128, 256], BF16, chan=ldc[i]) for i in range(2)]
            BCT = [kb.sb(st, f"s_BCT{i}", [128, 4, 128], BF16, chan=ldc[i]) for i in range(2)]
            dtc = [kb.sb(st, f"s_dt{i}", [128, 32], F32, chan=ldc[i]) for i in range(2)]
            u_ = kb.sb(st, "s_u", [128, 32], F32)
            ax_ = kb.sb(st, "s_ax", [128, 32], F32)
            ee_ = kb.sb(st, "s_ee", [128, 32], F32)
            ll_ = kb.sb(st, "s_ll", [128, 32], F32)
            dtp = [kb.sb(st, f"s_dtp{i}", [128, 32], F32) for i in range(2)]
            adt = [kb.sb(st, f"s_adt{i}", [128, 32], F32) for i in range(2)]
            rhsA = [kb.sb(st, f"s_rA{i}", [128, 8, 128], F32) for i in range(2)]
            dec = [kb.sb(st, f"s_dec{i}", [128, 8, 128], BF16) for i in range(2)]
            msc = [kb.sb(st, f"s_msc{i}", [128, 128], BF16) for i in range(2)]
            MT = [kb.sb(st, f"s_MT{i}", [128, 8, 128], BF16) for i in range(2)]
            csb = [kb.sb(st, f"s_csb{i}", [128, 16], F32) for i in range(2)]
            ecs = [kb.sb(st, f"s_ecs{i}", [128, 8], F32) for i in range(2)]
            tmc = [kb.sb(st, f"s_tmc{i}", [128, 8], F32) for i in range(2)]
            dte = [kb.sb(st, f"s_dte{i}", [128, 8], F32) for i in range(2)]
            cd = [kb.sb(st, f"s_cd{i}", [128, 8], F32) for i in range(2)]
            xdt = [kb.sb(st, f"s_xdt{i}", [128, 8, 64], BF16) for i in range(2)]
            xdte = [kb.sb(st, f"s_xdte{i}", [128, 8, 64], BF16) for i in range(2)]
            yo = [kb.sb(st, f"s_yo{i}", [128, 8, 64], F32) for i in range(2)]
            yc = [kb.sb(st, f"s_yc{i}", [128, 8, 64], F32) for i in range(2)]
            st1 = [kb.sb(st, f"s_st1{i}", [128, 8, 64], F32) for i in range(2)]
            kb.op("scalar", lambda e: e.activation(out=ab.h[:], in_=pb.h[:, pbo + PB_ALOG:pbo + PB_ALOG + 32], func=AF.Exp), reads=[pb], writes=[ab])
            kb.op("vector", lambda e: e.tensor_scalar(out=ab.h[:], in0=ab.h[:], scalar1=-1.0, scalar2=None, op0=ALU.mult), reads=[ab], writes=[ab])
            for d in range(2):
                for g in range(2):
                    kb.op("vector", lambda e, d=d, g=g: e.memset(state[d][g].h[:], 0.0), writes=[state[d][g]])
                    kb.op("vector", lambda e, d=d, g=g: e.memset(prevb[d][g].h[:], 0.0), writes=[prevb[d][g]])

            def pro(d, c, i):
                cs_ = slice(c * 128, (c + 1) * 128)
                kb.dma("sync", Xc[i].h[:].rearrange("p h f -> p (h f)"), Xtok.h[cs_, :], reads=[Xtok], writes=[Xc[i]], chan=ldc[i])
                kb.dma("sync", Bc[i].h[:], Btok.h[cs_, :], reads=[Btok], writes=[Bc[i]], chan=ldc[i])
                kb.dma("sync", BCT[i].h[:], xcT.h[1024:1536, cs_].rearrange("(j p) t -> p j t", p=128), reads=[xcT], writes=[BCT[i]], chan=ldc[i])
                kb.dma("sync", dtc[i].h[:], dtm.h[cs_, :], reads=[dtm], writes=[dtc[i]], chan=ldc[i])
                kb.op("vector", lambda e: e.tensor_tensor(out=u_.h[:], in0=dtc[i].h[:], in1=pb.h[:, pbo + PB_DTB:pbo + PB_DTB + 32], op=ALU.add),
                      reads=[dtc[i], pb], writes=[u_])
                kb.op("vector", lambda e: e.tensor_scalar(out=ax_.h[:], in0=u_.h[:], scalar1=-1.0, scalar2=None, op0=ALU.mult), reads=[u_], writes=[ax_])
                kb.op("vector", lambda e: e.tensor_tensor(out=ax_.h[:], in0=ax_.h[:], in1=u_.h[:], op=ALU.min), reads=[u_, ax_], writes=[ax_])
                kb.op("scalar", lambda e: e.activation(out=ee_.h[:], in_=ax_.h[:], func=AF.Exp), reads=[ax_], writes=[ee_])
                kb.op("scalar", lambda e: e.activation(out=ll_.h[:], in_=ee_.h[:], func=AF.Ln, bias=1.0, scale=1.0), reads=[ee_], writes=[ll_])
                kb.op("vector", lambda e: e.scalar_tensor_tensor(out=dtp[i].h[:], in0=u_.h[:], scalar=0.0, in1=ll_.h[:], op0=ALU.max, op1=ALU.add),
                      reads=[u_, ll_], writes=[dtp[i]])
                kb.op("vector", lambda e: e.tensor_tensor(out=adt[i].h[:], in0=dtp[i].h[:], in1=ab.h[:], op=ALU.mult), reads=[dtp[i], ab], writes=[adt[i]])

            PE_ = dbg.get("pool_eng", "gpsimd")

            def front(d, c, g, i, j):
                hs = d * 16 + g * 8
                p2 = ps2[j]
                kb.op(PE_, lambda e: e.tensor_tensor(
                    out=rhsA[j].h[:], in0=TRI[d].unsqueeze(1).broadcast_to([128, 8, 128]),
                    in1=adt[i].h[:, hs:hs + 8].unsqueeze(2).broadcast_to([128, 8, 128]), op=ALU.mult),
                    reads=[consts_f, adt[i]], writes=[rhsA[j]])
                def mmA(e):
                    rv = rhsA[j].h[:].rearrange("p h l -> p (h l)")
                    e.matmul(p2.h[:, 0:512], lhsT=STRICT[d], rhs=rv[:, 0:512], start=True, stop=True)
                    return e.matmul(p2.h[:, 512:1024], lhsT=STRICT[d], rhs=rv[:, 512:1024], start=True, stop=True)
                kb.op("tensor", mmA, reads=[rhsA[j], consts_f], writes=[p2])
                def mmB(e):
                    e.matmul(psm.h[:, 0:8], lhsT=TRI[d], rhs=adt[i].h[:, hs:hs + 8], start=True, stop=True)
                    e.matmul(psm.h[:, 8:16], lhsT=ones_f.h[:], rhs=adt[i].h[:, hs:hs + 8], start=True, stop=True)
                    return e.matmul(psm.h[:, 128:256], lhsT=BCT[i].h[:, g, :], rhs=BCT[i].h[:, 2 + g, :], start=True, stop=True)
                kb.op("tensor", mmB, reads=[adt[i], consts_f, ones_f, BCT[i]], writes=[psm])
                kb.op("scalar", lambda e: e.activation(out=dec[j].h[:].rearrange("p h l -> p (h l)"), in_=p2.h[:], func=AF.Exp),
                      reads=[p2], writes=[dec[j]])
                kb.op(PE_, lambda e: e.tensor_tensor(
                    out=xdt[j].h[:], in0=Xc[i].h[:, g * 8:(g + 1) * 8, :],
                    in1=dtp[i].h[:, hs:hs + 8].unsqueeze(2).broadcast_to([128, 8, 64]), op=ALU.mult),
                    reads=[Xc[i], dtp[i]], writes=[xdt[j]])

            def front2(d, c, g, i, j):
                hs = d * 16 + g * 8
                kb.op("vector", lambda e: e.tensor_tensor(out=msc[j].h[:], in0=psm.h[:, 128:256], in1=TRI[d], op=ALU.mult),
                      reads=[psm, consts_f], writes=[msc[j]])
                kb.op("vector", lambda e: e.tensor_copy(out=csb[j].h[:], in_=psm.h[:, 0:16]), reads=[psm], writes=[csb[j]])
                kb.op("vector", lambda e: e.tensor_tensor(out=MT[j].h[:], in0=dec[j].h[:],
                                                         in1=msc[j].h[:].unsqueeze(1).broadcast_to([128, 8, 128]), op=ALU.mult),
                      reads=[dec[j], msc[j]], writes=[MT[j]])
                kb.op("scalar", lambda e: e.activation(out=ecs[j].h[:], in_=csb[j].h[:, 0:8], func=AF.Exp), reads=[csb[j]], writes=[ecs[j]])
                kb.op("vector", lambda e: e.tensor_tensor(out=tmc[j].h[:], in0=csb[j].h[:, 8:16], in1=csb[j].h[:, 0:8], op=ALU.subtract),
                      reads=[csb[j]], writes=[tmc[j]])
                kb.op("scalar", lambda e: e.activation(out=dte[j].h[:], in_=tmc[j].h[:], func=AF.Exp), reads=[tmc[j]], writes=[dte[j]])
                kb.op("scalar", lambda e: e.activation(out=cd[j].h[:], in_=csb[j].h[:, 8:16], func=AF.Exp), reads=[csb[j]], writes=[cd[j]])
                kb.op(PE_, lambda e: e.tensor_tensor(out=xdte[j].h[:], in0=xdt[j].h[:],
                                                         in1=dte[j].h[:].unsqueeze(2).broadcast_to([128, 8, 64]), op=ALU.mult),
                      reads=[xdt[j], dte[j]], writes=[xdte[j]])

            def back(d, c, g, i, j):
                def mmY(e):
                    ins = None
                    for h in range(8):
                        ins = e.matmul(psy.h[:, h * 64:(h + 1) * 64], lhsT=MT[j].h[:, h, :], rhs=xdt[j].h[:, h, :], start=True, stop=True)
                    return ins
                kb.op("tensor", mmY, reads=[MT[j], xdt[j]], writes=[psy])
                kb.op("tensor", lambda e: e.matmul(pso.h[:], lhsT=BCT[i].h[:, 2 + g, :], rhs=prevb[d][g].h[:], start=True, stop=True),
                      reads=[BCT[i], prevb[d][g]], writes=[pso])
                kb.op("tensor", lambda e: e.matmul(pss_.h[:], lhsT=Bc[i].h[:, g * 128:(g + 1) * 128],
                                                  rhs=xdte[j].h[:].rearrange("p h f -> p (h f)"), start=True, stop=True),
                      reads=[Bc[i], xdte[j]], writes=[pss_])

            def back_dve(d, c, g, i, j):
                kb.op("vector", lambda e: e.tensor_tensor(out=yo[j].h[:], in0=pso.h[:].rearrange("p (h f) -> p h f", h=8),
                                                         in1=ecs[j].h[:].unsqueeze(2).broadcast_to([128, 8, 64]), op=ALU.mult),
                      reads=[pso, ecs[j]], writes=[yo[j]])
                yslot = ytot.h[:, c, g * 512:(g + 1) * 512].rearrange("p (h f) -> p h f", h=8)
                if d == 0:
                    kb.op("vector", lambda e: e.tensor_tensor(out=yslot, in0=yo[j].h[:],
                                                             in1=psy.h[:].rearrange("p (h f) -> p h f", h=8), op=ALU.add),
                          reads=[yo[j], psy], writes=[ytot])
                else:
                    kb.op("vector", lambda e: e.tensor_tensor(out=yc[j].h[:], in0=yo[j].h[:],
                                                             in1=psy.h[:].rearrange("p (h f) -> p h f", h=8), op=ALU.add),
                          reads=[yo[j], psy], writes=[yc[j]])
                    kb.op("vector", lambda e: e.tensor_tensor(out=yslot, in0=yc[j].h[:], in1=yslot, op=ALU.add),
                          reads=[yc[j], ytot], writes=[ytot])
                sg_ = state[d][g]
                kb.op("vector", lambda e: e.tensor_tensor(out=st1[j].h[:], in0=sg_.h[:],
                                                         in1=cd[j].h[:].unsqueeze(2).broadcast_to([128, 8, 64]), op=ALU.mult),
                      reads=[sg_, cd[j]], writes=[st1[j]])
                kb.op("vector", lambda e: e.tensor_tensor(out=sg_.h[:], in0=st1[j].h[:],
                                                         in1=pss_.h[:].rearrange("p (h f) -> p h f", h=8), op=ALU.add),
                      reads=[st1[j], pss_], writes=[sg_])
                bnd = (NCH // 2 - 1) if d == 0 else (NCH // 2)
                if c == bnd:
                    kb.op("vector", lambda e: e.tensor_scalar(out=sg_.h[:], in0=sg_.h[:], scalar1=flag_ap, scalar2=None, op0=ALU.mult),
                          reads=[sg_, flags], writes=[sg_])
                kb.op("vector", lambda e: e.tensor_copy(out=prevb[d][g].h[:].rearrange("p (h f) -> p h f", h=8), in_=sg_.h[:]),
                      reads=[sg_], writes=[prevb[d][g]])

            units = []
            vi = 0
            for d in range(2):
                chunks = range(NCH) if d == 0 else range(NCH - 1, -1, -1)
                for c in chunks:
                    for g in range(2):
                        units.append((d, c, g, vi % 2, len(units) % 2, g == 0))
                    vi += 1

            def emit_front1(u):
                d, c, g, i, j, first = u
                if first:
                    pro(d, c, i)
                front(d, c, g, i, j)
            emit_front1(units[0])
            front2(*units[0][:5])
            for n, u in enumerate(units):
                back(*u[:5])
                if n + 1 < len(units):
                    emit_front1(units[n + 1])
                back_dve(*u[:5])
                if n + 1 < len(units):
                    front2(*units[n + 1][:5])
            kb.barrier()
        with ExitStack() as st:
            ptT = [kb.psum(st, f"ptT{i}", [128, 512], BF16) for i in range(2)]
            psy = kb.psum(st, "psy2", [128, 512])
            zt = kb.sb(st, "p_zt", [128, 8, TT], BF16, chan=kb.new_chan("p_zt"))
            xx = kb.sb(st, "p_xx", [128, 8, TT], BF16, chan=zt.reg.chan)
            uu = [kb.sb(st, f"p_uu{i}", [128, TT], F32) for i in range(2)]
            sz = [kb.sb(st, f"p_sz{i}", [128, TT], F32) for i in range(2)]
            ug = kb.sb(st, "p_ug", [128, 8, TT], F32)
            sq = kb.sb(st, "p_sq", [128, 8, TT], BF16)
            sd = kb.sb(st, "p_sd", [128, TT], F32)
            rs = kb.sb(st, "p_rs", [128, TT], F32)
            og = kb.sb(st, "p_og", [128, 8, TT], BF16)
            dsk = lambda fc: pf.h[:, pfo + PF_DSK + fc:pfo + PF_DSK + fc + 1]
            sng = lambda fc: pf.h[:, pfo + PF_SNG + fc:pfo + PF_SNG + fc + 1]
            mv = mixT.h.rearrange("(k p) t -> p k t", p=128)
            for t in range(NT):
                ts_ = slice(t * TT, (t + 1) * TT)
                kb.dma("sync", zt.h[:], zT.h.rearrange("(c p) t -> p c t", p=128)[:, :, ts_], reads=[zT], writes=[zt], chan=zt.reg.chan)
                kb.dma("sync", xx.h[:], xcT.h[0:1024, :].rearrange("(c p) t -> p c t", p=128)[:, :, ts_], reads=[xcT], writes=[xx], chan=zt.reg.chan)
                for fc in range(8):
                    pT = ptT[fc % 2]
                    def tr(e, fc=fc, pT=pT):
                        ins = None
                        for tb in range(4):
                            ins = e.transpose(pT.h[:, tb * 128:(tb + 1) * 128], ytot.h[:, t * 4 + tb, fc * 128:(fc + 1) * 128], ident_bf.h[:])
                        return ins
                    kb.op("tensor", tr, reads=[ytot, ident_bf], writes=[pT])
                    u = uu[fc % 2]
                    s_ = sz[fc % 2]
                    kb.op("vector", lambda e, fc=fc, pT=pT, u=u: e.scalar_tensor_tensor(out=u.h[:], in0=xx.h[:, fc, :], scalar=dsk(fc), in1=pT.h[:],
                                                                                      op0=ALU.mult, op1=ALU.add), reads=[xx, pf, pT], writes=[u])
                    kb.op("scalar", lambda e, fc=fc, s_=s_: e.activation(out=s_.h[:], in_=zt.h[:, fc, :], func=AF.Silu), reads=[zt], writes=[s_])
                    kb.op("vector", lambda e, fc=fc, u=u, s_=s_: e.tensor_tensor(out=ug.h[:, fc, :], in0=u.h[:], in1=s_.h[:], op=ALU.mult),
                          reads=[u, s_], writes=[ug])
                    kb.op("scalar", lambda e, fc=fc: e.activation(out=sq.h[:, fc, :], in_=ug.h[:, fc, :], func=AF.Square), reads=[ug], writes=[sq])
                for g in range(2):
                    def mmN(e, g=g):
                        ins = None
                        for q in range(4):
                            ins = e.matmul(psy.h[:], lhsT=ones_bf.h[:], rhs=sq.h[:, g * 4 + q, :], start=(q == 0), stop=(q == 3))
                        return ins
                    kb.op("tensor", mmN, reads=[sq, ones_bf], writes=[psy])
                    kb.op("scalar", lambda e: e.activation(out=sd.h[:], in_=psy.h[:], func=AF.Sqrt, bias=EPS, scale=1.0 / 512), reads=[psy], writes=[sd])
                    kb.op("vector", lambda e: e.reciprocal(out=rs.h[:], in_=sd.h[:]), reads=[sd], writes=[rs])
                    for q in range(4):
                        fc = g * 4 + q
                        kb.op("vector", lambda e, fc=fc: e.scalar_tensor_tensor(out=og.h[:, fc, :], in0=ug.h[:, fc, :], scalar=sng(fc), in1=rs.h[:],
                                                                               op0=ALU.mult, op1=ALU.mult), reads=[ug, pf, rs], writes=[og])
                kb.dma("sync", mv[:, 0:8, ts_], og.h[:], reads=[og], writes=[mixT], chan=mixT.reg.chan)
            kb.barrier()
        sto.close()

    for l in range(NL):
        W = wsc[l]
        pfo = l * NPF
        pbo = l * NPB
        xsrc = xT_in if l == 0 else xTs
        lam_init = 0.8 - 0.6 * math.exp(-0.3 * l)

        with ExitStack() as st:
            new_psum(st)
            wbufs = [kb.sb(st, f"mw{i}", [128, KD * 512], BF16, chan=kb.new_chan(f"mw{i}")) for i in range(2)]
            psM = kb.ps()
            for b in range(24):
                buf = load_w(wbufs, W["mod"], b, KD * 512)
                wv = buf.h[:].rearrange("p (k c) -> p k c", k=KD)
                def mm(e, b=b, wv=wv):
                    ins = None
                    for cc in range(4):
                        j = b * 4 + cc
                        for k in range(KD):
                            ins = e.matmul(psM.h[:, 2 * j:2 * j + 2], lhsT=wv[:, k, cc * 128:(cc + 1) * 128],
                                           rhs=silc.h[:, 2 * k:2 * k + 2], start=(k == 0), stop=(k == KD - 1))
                    return ins
                kb.op("tensor", mm, reads=[buf, silc], writes=[psM])
            kb.op("vector", lambda e: e.tensor_tensor(
                out=modT.h[:], in0=psM.h[:, 0:192].rearrange("p (j s) -> p j s", s=2),
                in1=pf.h[:, pfo + PF_BMOD:pfo + PF_BMOD + 96].unsqueeze(2).broadcast_to([128, 96, 2]), op=ALU.add),
                reads=[psM, pf], writes=[modT])
            for i, (sc0, g0) in enumerate(((16, PF_N1), (64, PF_N2))):
                kb.op("vector", lambda e, i=i, sc0=sc0, g0=g0: e.scalar_tensor_tensor(
                    out=modA.h[:, i], in0=modT.h[:, sc0:sc0 + 16, :], scalar=1.0,
                    in1=pf.h[:, pfo + g0:pfo + g0 + 16].unsqueeze(2).broadcast_to([128, 16, 2]),
                    op0=ALU.add, op1=ALU.mult), reads=[modT, pf], writes=[modA])
            lw = kb.sb(st, "lw", [128, 128], F32)
            lp = pb.h[:, pbo + PB_LAM:pbo + PB_LAM + 256]
            kb.op("vector", lambda e: e.tensor_tensor(out=lw.h[:].rearrange("p (a b) -> p a b", a=2),
                                                     in0=lp.rearrange("p (a t b) -> p a t b", a=2, t=2)[:, :, 0, :],
                                                     in1=lp.rearrange("p (a t b) -> p a t b", a=2, t=2)[:, :, 1, :], op=ALU.mult),
                  reads=[pb], writes=[lw])
            kb.op("vector", lambda e: e.reduce_sum(out=lamt.h[:, 2:4], in_=lw.h[:].rearrange("p (a b) -> p a b", a=2), axis=AX.X),
                  reads=[lw], writes=[lamt])
            kb.op("scalar", lambda e: e.activation(out=lamt.h[:, 4:6], in_=lamt.h[:, 2:4], func=AF.Exp), reads=[lamt], writes=[lamt])
            kb.op("vector", lambda e: e.scalar_tensor_tensor(out=lamt.h[:, 0:1], in0=lamt.h[:, 5:6], scalar=-lam_init,
                                                            in1=lamt.h[:, 4:5], op0=ALU.add, op1=ALU.subtract),
                  reads=[lamt], writes=[lamt])
            kb.op("vector", lambda e: e.tensor_scalar(out=gsl.h[:], in0=pf.h[:, pfo + PF_SLG:pfo + PF_SLG + 1],
                                                     scalar1=(1.0 - lam_init), scalar2=None, op0=ALU.mult),
                  reads=[pf], writes=[gsl])
            kb.barrier()

        if l + 1 < NL:
            kb._wait(kb.engs["gpsimd"], kb.engs["tensor"].chan, kb.engs["tensor"].chan.count)
            emit_casts(l + 1, ("mod", "inw", "ints", "out", "gate", "up", "down"), fence=False)
        a1 = lambda k, s: modA.h[:, 0, k, s:s + 1]
        b1 = lambda k, s: modT.h[:, 0 + k, s:s + 1]
        g1 = lambda k, s: modT.h[:, 32 + k, s:s + 1]
        a2 = lambda k, s: modA.h[:, 1, k, s:s + 1]
        b2 = lambda k, s: modT.h[:, 48 + k, s:s + 1]
        g2 = lambda k, s: modT.h[:, 80 + k, s:s + 1]

        if not dbg.get("skip_mix"):
            with ExitStack() as st:
                new_psum(st)
                xt = kb.sb(st, "a_xt", [128, KD, TT], F32, chan=kb.new_chan("a_xt"))
                actT = kb.sb(st, "a_act", [128, KD, TT], BF16)
                sqt = (kb.sb(st, "a_sq", [128, KD, TT], BF16), kb.sb(st, "a_sd", [128, TT], F32), kb.sb(st, "a_rs", [128, TT], F32))
                tmp = [kb.sb(st, f"a_tmp{i}", [128, TT], F32) for i in range(2)]
                wbufs = [kb.sb(st, f"a_w{i}", [128, KD * 512], BF16, chan=kb.new_chan(f"a_w{i}")) for i in range(2)]
                wts = kb.sb(st, "a_wts", [128, KD * 800], BF16, chan=kb.new_chan("a_wts"))
                cst = kb.sb(st, "a_cos", [128, TT], F32, chan=kb.new_chan("a_cs"))
                snt = kb.sb(st, "a_sin", [128, TT], F32, chan=cst.reg.chan)
                stg_z = kb.sb(st, "a_sz", [128, 8, TT], BF16)
                stg_x = kb.sb(st, "a_sx", [128, 12, TT], BF16)
                stg_q = kb.sb(st, "a_sq4", [128, 4, TT], BF16)
                stg_k = kb.sb(st, "a_sk", [128, 2, TT], BF16)
                stg_dq = kb.sb(st, "a_sdq", [128, 4, TT], BF16)
                stg_dk = kb.sb(st, "a_sdk", [128, 4, TT], BF16)
                stg_dt = kb.sb(st, "a_sdt", [128, 4, 32], F32)
                stg_vg = kb.sb(st, "a_svg", [128, 4, 256], BF16)
                stg_vd = kb.sb(st, "a_svd", [128, 4, 512], BF16)
                h_sq = kb.sb(st, "a_hsq", [128, 1, TT], BF16)
                h_sd = kb.sb(st, "a_hsd", [128, TT], F32)
                h_rs = kb.sb(st, "a_hrs", [128, TT], F32)
                h_qn = kb.sb(st, "a_hqn", [128, TT], BF16)
                h_t1 = kb.sb(st, "a_ht1", [128, TT], F32)
                h_t2 = kb.sb(st, "a_ht2", [128, TT], F32)
                kb.dma("sync", wts.h[:], W["ints"].h[0], reads=[W["ints"]], writes=[wts], chan=wts.reg.chan)
                wtv = wts.h[:].rearrange("p (k c) -> p k c", k=KD)
                xsv = xsrc.h.rearrange("(k p) t -> p k t", p=128)
                for t in range(NT):
                    s = t // (NT // 2)
                    ts_ = slice(t * TT, (t + 1) * TT)
                    kb.dma("sync", xt.h[:], xsv[:, :, ts_], reads=[xsrc], writes=[xt], chan=xt.reg.chan)
                    kb.dma("sync", cst.h[:], cos_in.h[:, ts_], reads=[], writes=[cst], chan=cst.reg.chan)
                    kb.dma("sync", snt.h[:], sin_in.h[:, ts_], reads=[], writes=[snt], chan=cst.reg.chan)
                    norm_mod(xt, actT, sqt, lambda k: a1(k, s), lambda k: b1(k, s), tmp)
                    for b in range(9):
                        buf = load_w(wbufs, W["inw"], b, KD * 512)
                        wv = buf.h[:].rearrange("p (k c) -> p k c", k=KD)
                        for cc in range(4):
                            m = b * 4 + cc
                            if m >= 34:
                                break
                            ps = kb.ps()
                            def mm(e, wv=wv, cc=cc, ps=ps):
                                ins = None
                                for k in range(KD):
                                    ins = e.matmul(ps.h[:], lhsT=wv[:, k, cc * 128:(cc + 1) * 128], rhs=actT.h[:, k, :],
                                                   start=(k == 0), stop=(k == KD - 1))
                                return ins
                            kb.op("tensor", mm, reads=[buf, actT], writes=[ps])
                            if m < 8:
                                dst, j = stg_z, m
                            elif m < 20:
                                dst, j = stg_x, m - 8
                            elif m < 24:
                                dst, j = stg_q, m - 20
                            elif m < 26:
                                dst, j = stg_k, m - 24
                            elif m < 30:
                                dst, j = stg_dq, m - 26
                            else:
                                dst, j = stg_dk, m - 30
                            if 20 <= m < 26:
                                gcol = pf.h[:, pfo + (PF_QG if m < 24 else PF_KG):pfo + (PF_QG if m < 24 else PF_KG) + 1]
                                kb.op("scalar", lambda e, ps=ps: e.activation(out=h_sq.h[:, 0, :], in_=ps.h[:], func=AF.Square),
                                      reads=[ps], writes=[h_sq])
                                ps2 = kb.ps()
                                kb.op("tensor", lambda e, ps2=ps2: e.matmul(ps2.h[:], lhsT=ones_bf.h[:], rhs=h_sq.h[:, 0, :], start=True, stop=True),
                                      reads=[h_sq, ones_bf], writes=[ps2])
                                kb.op("scalar", lambda e, ps2=ps2: e.activation(out=h_sd.h[:], in_=ps2.h[:], func=AF.Sqrt, bias=EPS, scale=1.0 / 128),
                                      reads=[ps2], writes=[h_sd])
                                kb.op("vector", lambda e: e.reciprocal(out=h_rs.h[:], in_=h_sd.h[:]), reads=[h_sd], writes=[h_rs])
                                kb.op("vector", lambda e, ps=ps, gcol=gcol: e.scalar_tensor_tensor(
                                    out=h_qn.h[:], in0=ps.h[:], scalar=gcol, in1=h_rs.h[:], op0=ALU.mult, op1=ALU.mult),
                                    reads=[ps, h_rs, pf], writes=[h_qn])
                                ps3 = kb.ps()
                                kb.op("tensor", lambda e, ps3=ps3: e.matmul(ps3.h[:], lhsT=rm_bf.h[:], rhs=h_qn.h[:], start=True, stop=True),
                                      reads=[h_qn, rm_bf], writes=[ps3])
                                kb.op("vector", lambda e: e.tensor_tensor(out=h_t1.h[:], in0=h_qn.h[:], in1=cst.h[:], op=ALU.mult),
                                      reads=[h_qn, cst], writes=[h_t1])
                                kb.op("vector", lambda e, ps3=ps3: e.tensor_tensor(out=h_t2.h[:], in0=ps3.h[:], in1=snt.h[:], op=ALU.mult),
                                      reads=[ps3, snt], writes=[h_t2])
                                kb.op("vector", lambda e, dst=dst, j=j: e.tensor_tensor(out=dst.h[:, j, :], in0=h_t1.h[:], in1=h_t2.h[:], op=ALU.add),
                                      reads=[h_t1, h_t2], writes=[dst])
                            else:
                                kb.op("scalar", lambda e, ps=ps, dst=dst, j=j: e.copy(out=dst.h[:, j, :], in_=ps.h[:]),
                                      reads=[ps], writes=[dst])
                    for tb in range(4):
                        psA = kb.ps()
                        psB = kb.ps()
                        def mm(e, tb=tb, psA=psA, psB=psB):
                            ins = None
                            for k in range(KD):
                                ins = e.matmul(psA.h[:, 0:288], lhsT=actT.h[:, k, tb * 128:(tb + 1) * 128], rhs=wtv[:, k, 0:288],
                                               start=(k == 0), stop=(k == KD - 1))
                            for k in range(KD):
                                ins = e.matmul(psB.h[:, 0:512], lhsT=actT.h[:, k, tb * 128:(tb + 1) * 128], rhs=wtv[:, k, 288:800],
                                               start=(k == 0), stop=(k == KD - 1))
                            return ins
                        kb.op("tensor", mm, reads=[actT, wts], writes=[psA, psB])
                        kb.op("vector", lambda e, tb=tb, psA=psA: e.tensor_copy(out=stg_dt.h[:, tb, :], in_=psA.h[:, 0:32]),
                              reads=[psA], writes=[stg_dt])
                        kb.op("vector", lambda e, tb=tb, psA=psA: e.tensor_copy(out=stg_vg.h[:, tb, :], in_=psA.h[:, 32:288]),
                              reads=[psA], writes=[stg_vg])
                        kb.op("scalar", lambda e, tb=tb, psB=psB: e.copy(out=stg_vd.h[:, tb, :], in_=psB.h[:, 0:512]),
                              reads=[psB], writes=[stg_vd])
                    for stg, dst, nchk in ((stg_z, zT, 8), (stg_x, xbcT, 12), (stg_q, qT, 4), (stg_k, kT, 2), (stg_dq, dqT, 4), (stg_dk, dkT, 4)):
                        kb.dma("sync", dst.h.rearrange("(c p) t -> p c t", p=128)[:, :, ts_], stg.h[:], reads=[stg], writes=[dst], chan=dst.reg.chan)
                    for stg, dst in ((stg_dt, dtm), (stg_vg, vg), (stg_vd, vd)):
                        kb.dma("sync", dst.h[ts_, :].rearrange("(b p) f -> p b f", p=128), stg.h[:], reads=[stg], writes=[dst], chan=dst.reg.chan)
                kb.barrier()

            if not dbg.get("skip_ssd"):
                emit_ssd(l, pfo, pbo)
            else:
                zero_mix(0, 8)
            if not dbg.get("skip_attn"):
                emit_attn(l, pfo)
            else:
                zero_mix(8, 16)

        with ExitStack() as st:
            new_psum(st)
            xt = kb.sb(st, "f_xt", [128, KD, TT], F32, chan=kb.new_chan("f_xt"))
            actT = kb.sb(st, "f_act", [128, KD, TT], BF16, chan=kb.new_chan("f_act"))
            sqt = (kb.sb(st, "f_sq", [128, KD, TT], BF16), kb.sb(st, "f_sd", [128, TT], F32), kb.sb(st, "f_rs", [128, TT], F32))
            tmp = [kb.sb(st, f"f_tmp{i}", [128, TT], F32) for i in range(2)]
            fT = kb.sb(st, "f_fT", [128, KF, TT], BF16)
            sgs = [kb.sb(st, f"f_sg{i}", [128, TT], F32) for i in range(2)]
            wbufs = [kb.sb(st, f"f_w{i}", [128, KD * 512], BF16, chan=kb.new_chan(f"f_w{i}")) for i in range(4)]
            xsv = xsrc.h.rearrange("(k p) t -> p k t", p=128)
            xdv = xTs.h.rearrange("(k p) t -> p k t", p=128)
            mxv = mixT.h.rearrange("(k p) t -> p k t", p=128)
            for t in range(NT):
                s = t // (NT // 2)
                ts_ = slice(t * TT, (t + 1) * TT)
                kb.dma("sync", xt.h[:], xsv[:, :, ts_], reads=[xsrc], writes=[xt], chan=xt.reg.chan)
                if not dbg.get("skip_mix"):
                    kb.dma("sync", actT.h[:], mxv[:, :, ts_], reads=[mixT], writes=[actT], chan=actT.reg.chan)
                    for b in range(4):
                        buf = load_w(wbufs, W["out"], b, KD * 512)
                        wv = buf.h[:].rearrange("p (k c) -> p k c", k=KD)
                        for cc in range(4):
                            m = b * 4 + cc
                            ps = kb.ps()
                            def mm(e, wv=wv, cc=cc, ps=ps):
                                ins = None
                                for k in range(KD):
                                    ins = e.matmul(ps.h[:], lhsT=wv[:, k, cc * 128:(cc + 1) * 128], rhs=actT.h[:, k, :],
                                                   start=(k == 0), stop=(k == KD - 1))
                                return ins
                            kb.op("tensor", mm, reads=[buf, actT], writes=[ps])
                            kb.op("vector", lambda e, m=m, ps=ps: e.scalar_tensor_tensor(
                                out=xt.h[:, m, :], in0=ps.h[:], scalar=g1(m, s), in1=xt.h[:, m, :], op0=ALU.mult, op1=ALU.add),
                                reads=[ps, xt, modT], writes=[xt])
                norm_mod(xt, actT, sqt, lambda k: a2(k, s), lambda k: b2(k, s), tmp)
                for b in range(11):
                    bg = load_w(wbufs, W["gate"], b, KD * 512)
                    bu = load_w(wbufs, W["up"], b, KD * 512)
                    wg = bg.h[:].rearrange("p (k c) -> p k c", k=KD)
                    wu = bu.h[:].rearrange("p (k c) -> p k c", k=KD)
                    for cc in range(4):
                        j = b * 4 + cc
                        psg = kb.ps()
                        psu = kb.ps()
                        def mm(e, wg=wg, wu=wu, cc=cc, psg=psg, psu=psu):
                            ins = None
                            for k in range(KD):
                                ins = e.matmul(psg.h[:], lhsT=wg[:, k, cc * 128:(cc + 1) * 128], rhs=actT.h[:, k, :],
                                               start=(k == 0), stop=(k == KD - 1))
                            for k in range(KD):
                                ins = e.matmul(psu.h[:], lhsT=wu[:, k, cc * 128:(cc + 1) * 128], rhs=actT.h[:, k, :],
                                               start=(k == 0), stop=(k == KD - 1))
                            return ins
                        kb.op("tensor", mm, reads=[bg, bu, actT], writes=[psg, psu])
                        sg = sgs[j % 2]
                        kb.op("scalar", lambda e, psg=psg, sg=sg: e.activation(out=sg.h[:], in_=psg.h[:], func=AF.Silu),
                              reads=[psg], writes=[sg])
                        kb.op("vector", lambda e, psu=psu, sg=sg, j=j: e.tensor_tensor(out=fT.h[:, j, :], in0=sg.h[:], in1=psu.h[:], op=ALU.mult),
                              reads=[psu, sg], writes=[fT])
                for m in range(16):
                    buf = load_w(wbufs, W["down"], m, KF * 128)
                    wv = buf.h[:, 0:KF * 128].rearrange("p (k c) -> p k c", k=KF)
                    ps = kb.ps()
                    def mm(e, wv=wv, ps=ps):
                        ins = None
                        for k in range(KF):
                            ins = e.matmul(ps.h[:], lhsT=wv[:, k, :], rhs=fT.h[:, k, :], start=(k == 0), stop=(k == KF - 1))
                        return ins
                    kb.op("tensor", mm, reads=[buf, fT], writes=[ps])
                    kb.op("vector", lambda e, m=m, ps=ps: e.scalar_tensor_tensor(
                        out=xt.h[:, m, :], in0=ps.h[:], scalar=g2(m, s), in1=xt.h[:, m, :], op0=ALU.mult, op1=ALU.add),
                        reads=[ps, xt, modT], writes=[xt])
                kb.dma("sync", xdv[:, :, ts_], xt.h[:], reads=[xt], writes=[xTs], chan=xTs.reg.chan)
            kb.barrier()

    with ExitStack() as st:
        new_psum(st)
        xt = kb.sb(st, "o_xt", [128, KD, TT], F32, chan=kb.new_chan("o_xt"))
        yt = kb.sb(st, "o_yt", [128, KD, TT], F32)
        sqt = (kb.sb(st, "o_sq", [128, KD, TT], BF16), kb.sb(st, "o_sd", [128, TT], F32), kb.sb(st, "o_rs", [128, TT], F32))
        xsv = xTs.h.rearrange("(k p) t -> p k t", p=128)
        yv = yT_out.h.rearrange("(k p) t -> p k t", p=128)
        fg0 = NL * NPF
        for t in range(NT):
            ts_ = slice(t * TT, (t + 1) * TT)
            kb.dma("sync", xt.h[:], xsv[:, :, ts_], reads=[xTs], writes=[xt], chan=xt.reg.chan)
            rstd = rms_bcast(sqt, lambda k: xt.h[:, k, :], KD, D, [xt])
            for k in range(KD):
                kb.op("vector", lambda e, k=k: e.scalar_tensor_tensor(out=yt.h[:, k, :], in0=xt.h[:, k, :], scalar=pf.h[:, fg0 + k:fg0 + k + 1],
                                                                     in1=rstd.h[:], op0=ALU.mult, op1=ALU.mult),
                      reads=[xt, rstd, pf], writes=[yt])
            kb.dma("sync", yv[:, :, ts_], yt.h[:], reads=[yt], writes=[yT_out], chan=yT_out.reg.chan)
        kb.barrier()
    es.close()
    return nc


def _tile_w(w, cw):
    K, N = w.shape
    nk = K // 128
    nb = N // cw
    return np.ascontiguousarray(w.reshape(nk, 128, nb, cw).transpose(2, 1, 0, 3)).reshape(nb, 128, nk * cw)


def _pcol(v):
    return np.ascontiguousarray(v.reshape(-1, 128).T)


def _host_consts(T, SEG):
    i = np.arange(128)
    UI = (i[:, None] <= i[None, :]).astype(np.float32)
    LS = (i[:, None] > i[None, :]).astype(np.float32)
    LI = (i[:, None] >= i[None, :]).astype(np.float32)
    US = (i[:, None] < i[None, :]).astype(np.float32)
    Rm = np.zeros((128, 128), np.float32)
    for base in (0, 64):
        for m in range(32):
            Rm[base + m + 32, base + m] = -1.0
            Rm[base + m, base + m + 32] = 1.0
    ident = np.eye(128, dtype=np.float32)
    consts = np.stack([UI, LS, LI, US, Rm, ident], axis=1)
    def tables(S):
        t = np.arange(S)
        row = (t // 64).astype(np.float32)
        col = (t % 64).astype(np.float32)
        inv = (10000.0 ** (-np.arange(0, 64, 2, dtype=np.float32) / 64)).astype(np.float32)
        ar = row[None, :] * inv[:, None]
        ac = col[None, :] * inv[:, None]
        ang = np.concatenate([ar, ar, ac, ac], axis=0)
        return np.cos(ang).astype(np.float32), np.sin(ang).astype(np.float32)
    NAL = 2 * T - 128
    OFFA = T - 128
    v = np.arange(NAL)[None, :] - OFFA - np.arange(128)[:, None]
    alibi = (-np.abs(v)).astype(np.float32)
    return consts, tables, alibi


def _prep_weights(inp, NL):
    f = np.float32
    out = {}
    w_in = inp["w_in"]
    cols_w = np.concatenate([np.arange(OFF_Z, OFF_DT), np.arange(OFF_GQ, OFF_GV), np.arange(OFF_DQ, OFF_DV)])
    cols_t = np.concatenate([np.arange(OFF_DT, OFF_GQ), np.arange(OFF_GV, OFF_DQ), np.arange(OFF_DV, OFF_DV + 512)])
    inw = np.zeros((NL, D, 9 * 512), f)
    inw[:, :, :cols_w.size] = w_in[:NL][:, :, cols_w]
    out["w_inw_t"] = np.stack([_tile_w(inw[l], 512) for l in range(NL)])
    out["w_ints_t"] = np.stack([_tile_w(np.ascontiguousarray(w_in[l][:, cols_t]), 800)[0] for l in range(NL)])
    out["w_mod_t"] = np.stack([_tile_w(inp["w_mod"][l], 512) for l in range(NL)])
    out["w_out_t"] = np.stack([_tile_w(inp["w_out"][l], 512) for l in range(NL)])
    out["w_gate_t"] = np.stack([_tile_w(inp["w_gate"][l], 512) for l in range(NL)])
    out["w_up_t"] = np.stack([_tile_w(inp["w_up"][l], 512) for l in range(NL)])
    out["w_down_t"] = np.stack([_tile_w(inp["w_down"][l], 128) for l in range(NL)])
    pf = np.zeros((128, NL * NPF + 16), f)
    pb = np.zeros((128, NL * NPB), f)
    for l in range(NL):
        o = l * NPF
        pf[:, o + PF_N1:o + PF_N1 + 16] = _pcol(inp["norm1_g"][l])
        pf[:, o + PF_N2:o + PF_N2 + 16] = _pcol(inp["norm2_g"][l])
        pf[:, o + PF_BMOD:o + PF_BMOD + 96] = _pcol(inp["b_mod"][l])
        for k in range(5):
            pf[:, o + PF_CW + k * 12:o + PF_CW + (k + 1) * 12] = _pcol(inp["conv_w"][l][k])
        pf[:, o + PF_CB:o + PF_CB + 12] = _pcol(inp["conv_b"][l])
        pf[:, o + PF_SNG:o + PF_SNG + 8] = _pcol(inp["ssd_norm_g"][l])
        pf[:, o + PF_DSK:o + PF_DSK + 8] = _pcol(np.repeat(inp["d_skip"][l], 64))
        pf[:, o + PF_QG] = inp["q_norm_g"][l]
        pf[:, o + PF_KG] = inp["k_norm_g"][l]
        pf[:, o + PF_SLG] = inp["diff_subln_g"][l]
        ob = l * NPB
        pb[:, ob + PB_DTB:ob + PB_DTB + 32] = inp["dt_bias"][l].reshape(1, 32)
        pb[:, ob + PB_ALOG:ob + PB_ALOG + 32] = inp["a_log"][l].reshape(1, 32)
        pb[:, ob + PB_LAM:ob + PB_LAM + 256] = inp["diff_lambda"][l].reshape(1, 256)
    pf[:, NL * NPF:] = _pcol(inp["final_g"])
    out["pf"] = pf
    out["pb"] = pb
    return out


_CACHE = {}


def run(inp, SEG, NL, dbg=None, trace=False):
    T = 2 * SEG
    xp, xs = np.asarray(inp["x_prompt"]), np.asarray(inp["x_sample"])
    cp, cs = np.asarray(inp["c_prompt"]), np.asarray(inp["c_sample"])
    Bp, Bs = xp.shape[0], xs.shape[0]
    inp = {k: np.asarray(v) for k, v in inp.items()}
    items = [("p", i) for i in range(Bp)] + [("s", i) for i in range(0, Bs, 2)]
    assert len(items) <= NCORES
    shared = _prep_weights(inp, NL)
    consts, tables, alibi = _host_consts(T, SEG)
    cos_p, sin_p = tables(T)
    cos_s, sin_s = tables(SEG)
    shared["consts"] = consts
    shared["alibi"] = alibi
    in_maps = []
    for c in range(NCORES):
        kind, i = items[c] if c < len(items) else items[0]
        m = dict(shared)
        if kind == "p":
            m["xT"] = np.ascontiguousarray(xp[i].T)
            cc = np.stack([cp[i], cp[i]], axis=0)
            m["flags"] = np.tile(np.array([[1.0, 0.0]], np.float32), (128, 1))
            m["cosT"], m["sinT"] = cos_p, sin_p
        else:
            m["xT"] = np.ascontiguousarray(np.concatenate([xs[i], xs[i + 1]], axis=0).T)
            cc = np.stack([cs[i], cs[i + 1]], axis=0)
            m["flags"] = np.tile(np.array([[0.0, -30000.0]], np.float32), (128, 1))
            m["cosT"] = np.concatenate([cos_s, cos_s], axis=1)
            m["sinT"] = np.concatenate([sin_s, sin_s], axis=1)
        m["cT"] = np.ascontiguousarray(cc.reshape(2, KD, 128).transpose(2, 1, 0)).reshape(128, 32)
        in_maps.append(m)
    key = (T, NL, tuple(sorted((dbg or {}).items())))
    if key not in _CACHE:
        _CACHE[key] = build_program(T, NL, dbg)
    nc = _CACHE[key]
    res = run_bass_kernel_spmd(nc, in_maps, core_ids=list(range(NCORES)), trace=trace)
    yp = np.zeros_like(xp)
    ys = np.zeros_like(xs)
    for c, (kind, i) in enumerate(items):
        y = res.results[c]["yT"].T
        if kind == "p":
            yp[i] = y
        else:
            ys[i] = y[:SEG]
            ys[i + 1] = y[SEG:]
    return (yp, ys), res


def kernel(**inputs):
    (yp, ys), _ = run(inputs, 2048, 4)
    return yp, ys
```
